# Optimizing a Trainium2 kernel written in Bass

```python
import math
import jax, jax.numpy as jnp
from jax import lax
import numpy as np

D_MODEL = 1024
BATCH = 32
SEQ = 2048
DEPTH = 1

A_WIDTH = D_MODEL
A_HEADS = 8
A_HEAD_DIM = A_WIDTH // A_HEADS
CHUNK = 128
B_WIDTH = D_MODEL // 2
B_GROUP = 16
B_GROUPS = B_WIDTH // B_GROUP
STATE = 64
DT_MIN = 1e-3
DT_MAX = 1e-1
EPS = 1e-6
IN_COLS = 3 * A_WIDTH + 2 * B_WIDTH + 2 * D_MODEL
SPLITS = (A_WIDTH, 2 * A_WIDTH, 3 * A_WIDTH, 3 * A_WIDTH + B_WIDTH,
          3 * A_WIDTH + 2 * B_WIDTH, 3 * A_WIDTH + 2 * B_WIDTH + D_MODEL)

kernel_name = "hybrid_gmlp_s5_gated_block"


def rmsnorm(x, g):
    xf = x.astype(jnp.float32)
    y = xf * lax.rsqrt(jnp.mean(xf * xf, axis=-1, keepdims=True) + EPS)
    return (y * g.astype(jnp.float32)).astype(x.dtype)


def layernorm(x, g, b):
    xf = x.astype(jnp.float32)
    mu = jnp.mean(xf, axis=-1, keepdims=True)
    xc = xf - mu
    y = xc * lax.rsqrt(jnp.mean(xc * xc, axis=-1, keepdims=True) + EPS)
    return (y * g.astype(jnp.float32) + b.astype(jnp.float32)).astype(x.dtype)


def gmlp_spatial_gating(u, v, ln_g, ln_b, w_s, b_s):
    bsz, seq, _ = v.shape
    v = layernorm(v, ln_g, ln_b)
    vc = v.reshape(bsz, seq // CHUNK, CHUNK, A_HEADS, A_HEAD_DIM)
    causal = jnp.tril(jnp.ones((CHUNK, CHUNK), dtype=bool))
    w = jnp.where(causal[None], w_s, 0)
    mixed = jnp.einsum('hts,bcshd->bcthd', w, vc) + jnp.transpose(b_s)[:, :, None]
    return u * mixed.reshape(bsz, seq, A_WIDTH)


def s5_scan(xb, lam_re, lam_im, log_dt, b_re, b_im, c_re, c_im, d_skip):
    bsz, seq, _ = xb.shape
    f32 = jnp.float32
    u = xb.astype(f32).reshape(bsz, seq, B_GROUPS, B_GROUP)
    dt = jnp.exp(log_dt.astype(f32))[:, None]
    lr = lam_re.astype(f32)
    li = lam_im.astype(f32)
    mag = jnp.exp(lr * dt)
    ab_re = mag * jnp.cos(li * dt)
    ab_im = mag * jnp.sin(li * dt)
    den = lr * lr + li * li
    nr = ab_re - 1.0
    ni = ab_im
    k_re = ((nr * lr + ni * li) / den)[..., None]
    k_im = ((ni * lr - nr * li) / den)[..., None]
    br = b_re.astype(f32)
    bi = b_im.astype(f32)
    bb_re = k_re * br - k_im * bi
    bb_im = k_re * bi + k_im * br
    bu_re = jnp.einsum('gph,bsgh->bsgp', bb_re, u)
    bu_im = jnp.einsum('gph,bsgh->bsgp', bb_im, u)
    a_re = jnp.broadcast_to(ab_re[None, None], (1, seq, B_GROUPS, STATE))
    a_im = jnp.broadcast_to(ab_im[None, None], (1, seq, B_GROUPS, STATE))

    def combine(e1, e2):
        ar1, ai1, br1, bi1 = e1
        ar2, ai2, br2, bi2 = e2
        return (ar1 * ar2 - ai1 * ai2,
                ar1 * ai2 + ai1 * ar2,
                ar2 * br1 - ai2 * bi1 + br2,
                ar2 * bi1 + ai2 * br1 + bi2)

    _, _, h_re, h_im = lax.associative_scan(combine, (a_re, a_im, bu_re, bu_im), axis=1)
    y = (jnp.einsum('ghp,bsgp->bsgh', c_re.astype(f32), h_re)
         - jnp.einsum('ghp,bsgp->bsgh', c_im.astype(f32), h_im))
    y = y.reshape(bsz, seq, B_WIDTH) + d_skip.astype(f32) * xb.astype(f32)
    return y.astype(xb.dtype)


def setup_inputs(seed: int = 0) -> dict:
    key = jax.random.key(seed)
    ks = jax.random.split(key, 24)
    f32 = jnp.float32
    nrm = lambda k, shape, scale: (jax.random.normal(k, shape, f32) * scale)
    n_idx = jnp.arange(STATE, dtype=f32)
    lam_re = -0.5 + nrm(ks[7], (DEPTH, B_GROUPS, STATE), 0.01)
    lam_im = math.pi * n_idx[None, None, :] + nrm(ks[8], (DEPTH, B_GROUPS, STATE), 0.01)
    log_dt = jax.random.uniform(ks[9], (DEPTH, B_GROUPS), f32,
                                math.log(DT_MIN), math.log(DT_MAX))
    b_scale = (B_GROUP ** -0.5) / math.sqrt(2.0)
    c_scale = (STATE ** -0.5)
    return {
        "x": nrm(ks[0], (BATCH, SEQ, D_MODEL), 1.0),
        "norm_gain": 1.0 + nrm(ks[1], (DEPTH, D_MODEL), 0.01),
        "w_in": nrm(ks[2], (DEPTH, D_MODEL, IN_COLS), D_MODEL ** -0.5),
        "a_ln_gain": 1.0 + nrm(ks[3], (DEPTH, A_WIDTH), 0.01),
        "a_ln_bias": nrm(ks[4], (DEPTH, A_WIDTH), 0.01),
        "a_spatial": nrm(ks[5], (DEPTH, A_HEADS, CHUNK, CHUNK), CHUNK ** -0.5),
        "a_spatial_bias": 1.0 + nrm(ks[6], (DEPTH, A_HEADS, CHUNK), 0.1),
        "w_a_down": nrm(ks[10], (DEPTH, A_WIDTH, D_MODEL), A_WIDTH ** -0.5),
        "lambda_re": lam_re,
        "lambda_im": lam_im,
        "log_dt": log_dt,
        "b_re": nrm(ks[11], (DEPTH, B_GROUPS, STATE, B_GROUP), b_scale),
        "b_im": nrm(ks[12], (DEPTH, B_GROUPS, STATE, B_GROUP), b_scale),
        "c_re": nrm(ks[13], (DEPTH, B_GROUPS, B_GROUP, STATE), c_scale),
        "c_im": nrm(ks[14], (DEPTH, B_GROUPS, B_GROUP, STATE), c_scale),
        "d_skip": nrm(ks[15], (DEPTH, B_WIDTH), 1.0),
        "w_glu": nrm(ks[16], (DEPTH, B_WIDTH, 2 * B_WIDTH), B_WIDTH ** -0.5),
        "w_b_down": nrm(ks[17], (DEPTH, B_WIDTH, D_MODEL), B_WIDTH ** -0.5),
        "w_out": nrm(ks[18], (DEPTH, D_MODEL, D_MODEL), D_MODEL ** -0.5),
        "final_gain": 1.0 + nrm(ks[19], (D_MODEL,), 0.01),
    }


def reference(x, norm_gain, w_in, a_ln_gain, a_ln_bias, a_spatial, a_spatial_bias, w_a_down,
              lambda_re, lambda_im, log_dt, b_re, b_im, c_re, c_im, d_skip, w_glu, w_b_down,
              w_out, final_gain):
    h = x
    for l in range(DEPTH):
        xn = rmsnorm(h, norm_gain[l])
        proj = jnp.einsum('bsd,dc->bsc', xn, w_in[l])
        u_a, v_a, z_a, x_b, z_b, g_a, g_b = jnp.split(proj, SPLITS, axis=-1)
        y_a = gmlp_spatial_gating(jax.nn.gelu(u_a), jax.nn.gelu(v_a), a_ln_gain[l], a_ln_bias[l],
                                  a_spatial[l], a_spatial_bias[l])
        y_a = jnp.einsum('bsc,cd->bsd', y_a * jax.nn.silu(z_a), w_a_down[l])
        y_b = s5_scan(x_b, lambda_re[l], lambda_im[l], log_dt[l], b_re[l], b_im[l],
                      c_re[l], c_im[l], d_skip[l])
        glu_p, glu_q = jnp.split(jnp.einsum('bsc,ce->bse', jax.nn.gelu(y_b), w_glu[l]), 2, axis=-1)
        y_b = glu_p * jax.nn.sigmoid(glu_q)
        y_b = jnp.einsum('bsc,cd->bsd', y_b * jax.nn.silu(z_b), w_b_down[l])
        merged = jax.nn.sigmoid(g_a) * y_a + jax.nn.sigmoid(g_b) * y_b
        h = h + jnp.einsum('bsd,de->bse', merged, w_out[l])
    return rmsnorm(h, final_gain)
```

```python
import numpy as np
from contextlib import ExitStack
import concourse.bass as bass
import concourse.mybir as mybir
from concourse.bass_utils import run_bass_kernel_spmd

F32 = mybir.dt.float32
BF16 = mybir.dt.bfloat16
AF = mybir.ActivationFunctionType
ALU = mybir.AluOpType

NCORES = 8
D = 1024
TOK = 8192
ST = 512
NST = TOK // ST
EPS = 1e-6
NR = 11
SEQ_ST = 2048 // ST


class Buf:
    def __init__(self, name=""):
        self.name = name
        self.writers = []
        self.readers = []
        self.phase_deps = []
        self.sem = None
        self.dma_cnt = 0


class Ins:
    __slots__ = ("eng", "fn", "deps", "signal", "idx", "is_dma", "sembuf", "semval", "cnt")

    def __init__(self, eng, fn, idx, is_dma=False):
        self.eng = eng
        self.fn = fn
        self.deps = []
        self.signal = False
        self.idx = idx
        self.is_dma = is_dma
        self.sembuf = None
        self.semval = 0
        self.cnt = 0


ENGS = ("pe", "act", "dve", "pool", "sp")


class Prog:
    def __init__(self, nc):
        self.nc = nc
        self.streams = {e: [] for e in ENGS}
        self.n = 0
        self.stack = ExitStack()
        self.dmas_since_barrier = []
        self.out_stores = []

    def sbuf(self, name, shape, dtype):
        return self.stack.enter_context(self.nc.sbuf_tensor(name, list(shape), dtype))

    def psum(self, name, shape, dtype=F32):
        return self.stack.enter_context(self.nc.psum_tensor(name, list(shape), dtype))

    def _reduce(self, deps):
        best = {}
        for d in deps:
            key = ("dma", id(d.sembuf)) if d.is_dma else d.eng
            if key not in best or best[key].idx < d.idx:
                best[key] = d
        return list(best.values())

    def op(self, eng, fn, reads=(), writes=(), dma=False, sembuf=None, deps=()):
        ins = Ins(eng, fn, self.n, is_dma=dma)
        self.n += 1
        dl = list(deps)
        for b in reads:
            dl.extend(b.writers)
        for b in writes:
            if b.readers:
                b.phase_deps = self._reduce(b.readers + b.writers)
                b.readers = []
                b.writers = []
            dl.extend(b.phase_deps)
        dl = [d for d in self._reduce(dl) if d is not ins]
        for d in dl:
            d.signal = True
        ins.deps = dl
        for b in reads:
            b.readers.append(ins)
        for b in writes:
            b.writers.append(ins)
        if dma:
            sb = sembuf if sembuf is not None else (writes[0] if writes else reads[0])
            ins.sembuf = sb
            sb.dma_cnt += 1
            ins.semval = 16 * sb.dma_cnt
            self.dmas_since_barrier.append(ins)
        self.streams[eng].append(ins)
        return ins

    def barrier(self):
        last = []
        for e in ENGS:
            for ins in reversed(self.streams[e]):
                if not ins.is_dma and ins.fn is not None:
                    last.append(ins)
                    break
        deps = self._reduce(last + self.dmas_since_barrier)
        self.dmas_since_barrier = []
        for e in ENGS:
            self.op(e, None, deps=deps)

    def finalize(self):
        nc = self.nc
        st = self.stack
        engsem = {e: st.enter_context(nc.semaphore("s_" + e)) for e in ENGS}
        for e in ENGS:
            for ins in self.streams[e]:
                if ins.is_dma and ins.sembuf.sem is None:
                    ins.sembuf.sem = st.enter_context(nc.semaphore("d%d" % ins.idx))
        for e in ENGS:
            c = 0
            for ins in self.streams[e]:
                if ins.is_dma or ins.fn is None:
                    continue
                if ins.signal:
                    c += 1
                    ins.cnt = c
        block = st.enter_context(nc.Block())

        def run(engname):
            def body(eng):
                waited = {}
                for ins in self.streams[engname]:
                    for d in ins.deps:
                        if d.is_dma:
                            sem, val = d.sembuf.sem, d.semval
                        else:
                            sem, val = engsem[d.eng], d.cnt
                        k = id(sem)
                        if waited.get(k, 0) >= val:
                            continue
                        waited[k] = val
                        eng.wait_ge(sem, val)
                    if ins.fn is None:
                        continue
                    r = ins.fn(eng)
                    if ins.is_dma:
                        r.then_inc(ins.sembuf.sem, 16)
                    elif ins.signal:
                        r.then_inc(engsem[engname], 1)
            return body

        block.tensor(run("pe"))
        block.scalar(run("act"))
        block.vector(run("dve"))
        block.gpsimd(run("pool"))
        block.sync(run("sp"))

    def close(self):
        self.stack.close()


BLK_U, BLK_V, BLK_ZA, BLK_XB, BLK_ZB, BLK_GA, BLK_GB = 0, 4, 8, 12, 14, 16, 20
BLK_AD, BLK_GLU, BLK_BD, BLK_WO = 24, 28, 32, 36
NBLK = 40
SEQ = (list(range(BLK_V, BLK_V + 4)) + list(range(BLK_U, BLK_U + 4)) + list(range(BLK_ZA, BLK_ZA + 4))
       + [BLK_XB, BLK_XB + 1, BLK_ZB, BLK_ZB + 1]
       + [BLK_GA, BLK_GA + 1, BLK_GA + 2, BLK_GA + 3, BLK_AD, BLK_AD + 1, BLK_AD + 2, BLK_AD + 3]
       + [BLK_GLU + 2, BLK_GLU, BLK_GLU + 3, BLK_GLU + 1]
       + [BLK_GB, BLK_GB + 1, BLK_BD, BLK_BD + 1, BLK_GB + 2, BLK_GB + 3, BLK_BD + 2, BLK_BD + 3]
       + list(range(BLK_WO, BLK_WO + 4)))
assert len(SEQ) == NBLK and sorted(SEQ) == list(range(NBLK))
POS = {b: i for i, b in enumerate(SEQ)}


def build_program(mode="full", nst=NST, upto=99):
    nc = bass.Bass("TRN2", target_bir_lowering=False)
    dt = lambda name, shape, dtype=F32, kind="ExternalInput": nc.dram_tensor(name, list(shape), dtype, kind=kind).ap()
    x_d = dt("x", [TOK, D])
    y_d = dt("y", [TOK, D], kind="ExternalOutput")
    norm_gain = dt("norm_gain", [D])
    w_in = dt("w_in", [D, 6144])
    a_ln_gain = dt("a_ln_gain", [D])
    a_ln_bias = dt("a_ln_bias", [D])
    a_spatial = dt("a_spatial", [8, 128, 128])
    a_sbias = dt("a_spatial_bias", [8, 128])
    w_a_down = dt("w_a_down", [D, D])
    lam_re = dt("lambda_re", [32, 64])
    lam_im = dt("lambda_im", [32, 64])
    log_dt = dt("log_dt", [32])
    b_re = dt("b_re", [32, 64, 16])
    b_im = dt("b_im", [32, 64, 16])
    c_re = dt("c_re", [32, 16, 64])
    c_im = dt("c_im", [32, 16, 64])
    d_skip = dt("d_skip", [512])
    w_glu = dt("w_glu", [512, 1024])
    w_b_down = dt("w_b_down", [512, D])
    w_out = dt("w_out", [D, D])
    final_gain = dt("final_gain", [D])
    wsc = dt("wsc", [NBLK, 128, 2048], BF16, kind=("Internal" if mode == "full" else "ExternalOutput"))

    P = Prog(nc)
    op = P.op

    idb = P.sbuf("idb", [128, 128], BF16)
    WsT = P.sbuf("WsT", [128, 8, 128], BF16)
    lng_bc = P.sbuf("lng_bc", [128, D], F32)
    lnb_bc = P.sbuf("lnb_bc", [128, D], BF16)
    fg_bc = P.sbuf("fg_bc", [128, D], F32)
    EC = P.sbuf("EC", [128, 16, 64], F32)
    ES = P.sbuf("ES", [128, 16, 64], F32)
    RHO = P.sbuf("RHO", [128, 16], F32)
    WS = P.sbuf("WS", [128, 4, 8, 2, 128], BF16)
    VV = P.sbuf("VV", [128, 16, 8, 2, 32], BF16)
    Kc = P.sbuf("Kc", [128, 4, 8, 128], BF16)
    mhalf = P.sbuf("mhalf", [128, 1], F32)
    gain_t = P.sbuf("gain_t", [128, 8], F32)
    bsrow = P.sbuf("bsrow", [1, 8, 512], BF16)
    ones1 = P.sbuf("ones1", [1, 128], BF16)
    B_const = Buf("const")

    xin = [P.sbuf("xin%d" % i, [128, D], F32) for i in range(2)]
    b_xin = [Buf("xin%d" % i) for i in range(2)]
    xres = xin
    b_xres = b_xin
    xs = [P.sbuf("xs%d" % i, [128, D], BF16) for i in range(4)]
    b_xs = [Buf() for _ in range(4)]
    stat = P.sbuf("stat", [128, 64], F32)
    P_bn = [P.sbuf("bn%d" % i, [128, 2, 6], F32) for i in range(4)]
    P_mv = [P.sbuf("mv%d" % i, [128, 4], F32) for i in range(4)]
    xnT2 = [P.sbuf("xnT0", [128, 8, ST], BF16)] * 2
    b_xnT2 = [Buf("xnT0")] * 2
    gusz = P.sbuf("gusz", [128, 8, ST], BF16)
    b_gusz = [Buf() for _ in range(8)]
    vn = P.sbuf("vn", [128, 4, D], BF16)
    b_vn = [Buf() for _ in range(4)]
    ub = P.sbuf("ub", [128, 4, ST], BF16)
    b_ub = [Buf() for _ in range(4)]
    szb = P.sbuf("szb", [128, 4, ST], BF16)
    b_szb = [Buf() for _ in range(4)]
    tga = P.sbuf("tga", [128, 8, ST], BF16)
    b_tga = [Buf() for _ in range(8)]
    hsh = P.sbuf("hsh", [128, 16, 2, 64], BF16)
    b_hsh = [Buf() for _ in range(4)]
    carry = P.sbuf("carry", [128, 16, 2], F32)
    b_carry = [Buf() for _ in range(4)]
    gy = ub
    b_gy = b_ub
    th = [P.sbuf("th%d" % i, [128, ST], BF16) for i in range(2)]
    b_th = [Buf() for _ in range(2)]
    szt = th
    b_szt = b_th
    tq = [P.sbuf("tq%d" % i, [128, ST], BF16) for i in range(2)]
    b_tq = [Buf() for _ in range(2)]
    ybp = szb
    b_ybp = b_szb
    yag = vn[:].rearrange("p c (h t) -> p (c h) t", h=2)
    b_yag = [b_vn[i // 2] for i in range(8)]
    tyb = tq
    b_tyb = b_tq
    s5u = [P.sbuf("s5u%d" % i, [128, 512], F32) for i in range(2)]
    b_s5u = [Buf() for _ in range(2)]
    gt2 = [P.sbuf("gt2_0", [128, 512], F32)] * 2
    b_gt2 = [Buf()] * 2
    ring = [P.sbuf("ring%d" % i, [128, 2048], BF16) for i in range(NR)]
    b_ring = [Buf("ring%d" % i) for i in range(NR)]
    SCR_N = 6144
    scr = P.sbuf("scr", [128, SCR_N], F32)
    gv = [scr[:, 0:1024], scr[:, 5120:6144]]
    b_gv = [Buf() for _ in range(2)]
    s5t = [scr[:, 1024 + 512 * i:1536 + 512 * i] for i in range(4)]
    b_s5t = [Buf() for _ in range(4)]
    ost = [scr[:, 3072:4096], scr[:, 4096:5120]]
    b_ost = [Buf() for _ in range(2)]
    psA = P.psum("psA", [128, 7 * 512], F32)
    psT = P.psum("psT", [128, 1024], BF16)
    b_bank = [Buf("bank%d" % i) for i in range(7)]
    b_psT = Buf("psT")
    bank = lambda i: psA[:, 512 * i:512 * (i + 1)]
    rr = {"b": 0, "p": 0}

    def next_bank():
        if rr.get("avoid"):
            cand = [b for b in range(7) if b not in rr["avoid"]]
            i = cand[rr["b"] % len(cand)]
            rr["b"] += 1
            return i
        if rr.get("hi", 0) > 0:
            rr["hi"] -= 1
            i = 4 + rr["b"] % 3
            rr["b"] += 1
            return i
        i = rr["b"] % 7
        rr["b"] += 1
        return i

    def next_pair():
        i = (rr["p"] % 3) * 2
        rr["p"] += 1
        return i

    so = {"o": 0}

    def salloc(n):
        o = so["o"]
        so["o"] += n
        assert so["o"] <= SCR_N
        return scr[:, o:o + n]

    B_s5 = Buf("s5in")
    DS, LR, LI, LDT = scr[:, 6140:6144], scr[:, 6124:6140], scr[:, 6108:6124], scr[:, 6092:6108]
    BR, BI = scr[:, 5836:6092], scr[:, 5580:5836]
    for m in range(2):
        sl = slice(64 * m, 64 * m + 64)
        op("act", lambda e, m=m, sl=sl: e.dma_start(out=LR[sl, :], in_=lam_re.rearrange("(q m) p -> m p q", m=2)[m],
                                                  allow_slow_non_contiguous=True), writes=[B_s5], dma=True)
        op("act", lambda e, m=m, sl=sl: e.dma_start(out=LI[sl, :], in_=lam_im.rearrange("(q m) p -> m p q", m=2)[m],
                                                  allow_slow_non_contiguous=True), writes=[B_s5], dma=True)
        op("act", lambda e, m=m, sl=sl: e.dma_start(out=LDT[sl, :],
                                                  in_=log_dt.rearrange("(q m) -> m q", m=2)[m:m + 1].to_broadcast([64, 16]),
                                                  allow_slow_non_contiguous=True), writes=[B_s5], dma=True)
    BR3 = BR.rearrange("p (q h) -> p q h", q=16)
    BI3 = BI.rearrange("p (q h) -> p q h", q=16)
    for m in range(2):
        sl = slice(64 * m, 64 * m + 64)
        op("act", lambda e, m=m, sl=sl: e.dma_start(out=BR3[sl], in_=b_re.rearrange("(q m) p h -> m p q h", m=2)[m]),
           writes=[B_s5], dma=True)
        op("act", lambda e, m=m, sl=sl: e.dma_start(out=BI3[sl], in_=b_im.rearrange("(q m) p h -> m p q h", m=2)[m]),
           writes=[B_s5], dma=True)
    op("act", lambda e: e.dma_start(out=DS, in_=d_skip.rearrange("(T p) -> p T", p=128), allow_slow_non_contiguous=True),
       writes=[B_s5], dma=True)

    idf = salloc(128)
    B_id = Buf("id")
    op("pool", lambda e: e.memset(idf, 1.0), writes=[B_id])
    op("pool", lambda e: e.affine_select(out=idf, in_=idf, pattern=[[1, 128]], compare_op=ALU.is_equal,
                                         fill=0.0, base=0, channel_multiplier=-1), reads=[B_id], writes=[B_id])
    op("dve", lambda e: e.tensor_copy(out=idb[:], in_=idf), reads=[B_id], writes=[B_const])
    op("dve", lambda e: e.memset(mhalf[:], -0.5), writes=[B_const])

    B_bc = Buf("bc")
    lnb32 = salloc(1024)
    bs32 = salloc(1024)
    op("sp", lambda e: e.dma_start(out=lng_bc[:], in_=a_ln_gain.unsqueeze(0).to_broadcast([128, D])), writes=[B_bc], dma=True)
    op("sp", lambda e: e.dma_start(out=fg_bc[:], in_=final_gain.unsqueeze(0).to_broadcast([128, D])), writes=[B_bc], dma=True)
    op("sp", lambda e: e.dma_start(out=lnb32, in_=a_ln_bias.unsqueeze(0).to_broadcast([128, D])), writes=[B_bc], dma=True)
    op("sp", lambda e: e.dma_start(out=bs32, in_=a_sbias.rearrange("h t -> (h t)").unsqueeze(0).to_broadcast([128, D])),
       writes=[B_bc], dma=True)
    op("dve", lambda e: e.tensor_copy(out=lnb_bc[:], in_=lnb32), reads=[B_bc], writes=[B_const])
    for h in range(8):
        for C in range(4):
            op("dve", lambda e, h=h, C=C: e.tensor_copy(
                out=bsrow[0:1, h, 128 * C:128 * C + 128].rearrange("p (j c) -> p j c", j=8),
                in_=bs32[0:1, 128 * h:128 * h + 128].rearrange("p (c j) -> p j c", j=8)), reads=[B_bc], writes=[B_const])
    op("dve", lambda e: e.memset(ones1[:], 1.0), writes=[B_const])

    Wn = salloc(1024)
    Wp = salloc(1024)
    B_w = Buf("wn")
    Wn3 = Wn.rearrange("p (h s) -> p h s", h=8)
    Wp3 = Wp.rearrange("p (h s) -> p h s", h=8)
    op("sp", lambda e: e.dma_start(out=Wn3, in_=a_spatial.rearrange("h t s -> t h s")), writes=[B_w], dma=True)
    op("pool", lambda e: e.affine_select(out=Wn3, in_=Wn3, pattern=[[0, 8], [-1, 128]], compare_op=ALU.is_ge,
                                         fill=0.0, base=0, channel_multiplier=1), reads=[B_w], writes=[B_w])
    B_wp = Buf("wp")
    for h in range(8):
        op("dve", lambda e, h=h: e.tensor_copy(out=Wp3[:, h, :].rearrange("p (j c) -> p j c", j=8),
                                               in_=Wn3[:, h, :].rearrange("p (c j) -> p j c", j=8)),
           reads=[B_w], writes=[B_wp])
    for h in range(8):
        bk = next_bank()
        op("pe", lambda e, h=h, bk=bk: e.transpose(bank(bk)[:, 0:128], Wp3[:, h, :], idf), reads=[B_wp, B_id], writes=[b_bank[bk]])
        op("dve", lambda e, h=h, bk=bk: e.tensor_copy(out=WsT[:, h, :].rearrange("p (j c) -> p j c", j=8),
                                                      in_=bank(bk)[:, 0:128].rearrange("p (c j) -> p j c", j=8)),
           reads=[b_bank[bk]], writes=[B_const])

    P.barrier()
    so["o"] = 128

    op("sp", lambda e: e.dma_start(out=gain_t[:], in_=norm_gain.rearrange("(k p) -> p k", p=128), allow_slow_non_contiguous=True),
       writes=[B_const], dma=True, sembuf=B_s5)
    b_wblk = [Buf("wblk%d" % i) for i in range(NBLK)]
    for blk in SEQ:
        if blk < BLK_AD:
            src = w_in.rearrange("(k p) c -> p k c", p=128)[:, :, 256 * blk:256 * blk + 256]
            nk = 8
        elif blk < BLK_GLU:
            c0 = 256 * (blk - BLK_AD)
            src = w_a_down.rearrange("(k p) c -> p k c", p=128)[:, :, c0:c0 + 256]
            nk = 8
        elif blk < BLK_BD:
            c0 = 256 * (blk - BLK_GLU)
            src = w_glu.rearrange("(k p) c -> p k c", p=128)[:, :, c0:c0 + 256]
            nk = 4
        elif blk < BLK_WO:
            c0 = 256 * (blk - BLK_BD)
            src = w_b_down.rearrange("(k p) c -> p k c", p=128)[:, :, c0:c0 + 256]
            nk = 4
        else:
            c0 = 256 * (blk - BLK_WO)
            src = w_out.rearrange("(k p) c -> p k c", p=128)[:, :, c0:c0 + 256]
            nk = 8
        w = nk * 256
        ins_ = op("pool", lambda e, blk=blk, w=w, nk=nk, src=src: e.dma_start(
            out=wsc[blk, :, 0:w].rearrange("p (k c) -> p k c", k=nk), in_=src), writes=[b_wblk[blk]], dma=True)
        P.dmas_since_barrier.remove(ins_)

    P.barrier()
    so["o"] = 128

    def t16(n=16):
        return salloc(n)
    DT, X1, MAG, TH = [t16() for _ in range(4)]
    B_t = Buf("s5tmp")

    def tt(eng, out, a, b, o):
        op(eng, lambda e: e.tensor_tensor(out=out, in0=a, in1=b, op=o), reads=[B_s5, B_t], writes=[B_t])

    def act(out, in_, func, scale=1.0):
        op("act", lambda e: e.activation(out=out, in_=in_, func=func, scale=scale), reads=[B_s5, B_t], writes=[B_t])

    def tsc(out, a, s1, s2, o0, o1):
        op("dve", lambda e: e.tensor_scalar(out=out, in0=a, scalar1=s1, scalar2=s2, op0=o0, op1=o1),
           reads=[B_s5, B_t], writes=[B_t])

    act(DT, LDT, AF.Exp)
    tt("dve", X1, LR, DT, ALU.mult)
    act(MAG, X1, AF.Exp)
    op("act", lambda e: e.activation(out=RHO[:], in_=X1, func=AF.Exp, scale=8.0), reads=[B_t], writes=[B_const])
    tt("dve", TH, LI, DT, ALU.mult)
    SN, CS, SH, T1, T2, T3 = [t16() for _ in range(6)]
    act(SN, TH, AF.Sin, 1.0 / 16.0)
    act(SH, TH, AF.Sin, 1.0 / 32.0)
    tt("dve", T1, SH, SH, ALU.mult)
    tsc(CS, T1, -2.0, 1.0, ALU.mult, ALU.add)

    def csquare(c, s):
        tt("dve", T1, c, c, ALU.mult)
        tt("dve", T2, s, s, ALU.mult)
        tt("dve", T3, c, s, ALU.mult)
        tt("dve", c, T1, T2, ALU.subtract)
        tsc(s, T3, 2.0, None, ALU.mult, ALU.bypass)

    for _ in range(4):
        csquare(CS, SN)
    AR, AI = t16(), t16()
    tt("dve", AR, MAG, CS, ALU.mult)
    tt("dve", AI, MAG, SN, ALU.mult)
    NRr, DEN, RDEN, KR, KI = [t16() for _ in range(5)]
    tsc(NRr, AR, -1.0, None, ALU.add, ALU.bypass)
    tt("dve", T1, LR, LR, ALU.mult)
    tt("dve", T2, LI, LI, ALU.mult)
    tt("dve", DEN, T1, T2, ALU.add)
    op("dve", lambda e: e.reciprocal(out=RDEN, in_=DEN), reads=[B_t], writes=[B_t])
    tt("dve", T1, NRr, LR, ALU.mult)
    tt("dve", T2, AI, LI, ALU.mult)
    tt("dve", T3, T1, T2, ALU.add)
    tt("dve", KR, T3, RDEN, ALU.mult)
    tt("dve", T1, AI, LR, ALU.mult)
    tt("dve", T2, NRr, LI, ALU.mult)
    tt("dve", T3, T1, T2, ALU.subtract)
    tt("dve", KI, T3, RDEN, ALU.mult)
    BBR, BBI, U1, U2 = [salloc(256) for _ in range(4)]
    v3 = lambda a: a.rearrange("p (q h) -> p q h", q=16)
    bq = lambda a, n=16: a.unsqueeze(2).to_broadcast([128, 16, n])
    tt("dve", v3(U1), BR3, bq(KR), ALU.mult)
    tt("dve", v3(U2), BI3, bq(KI), ALU.mult)
    tt("dve", v3(BBR), v3(U1), v3(U2), ALU.subtract)
    tt("dve", v3(U1), BI3, bq(KR), ALU.mult)
    tt("dve", v3(U2), BR3, bq(KI), ALU.mult)
    tt("dve", v3(BBI), v3(U1), v3(U2), ALU.add)
    PWR = salloc(144)
    PWI = salloc(144)
    pw = lambda a, n: a[:, 16 * n:16 * n + 16]
    op("dve", lambda e: e.memset(pw(PWR, 0), 1.0), reads=[B_t], writes=[B_t])
    op("dve", lambda e: e.memset(pw(PWI, 0), 0.0), reads=[B_t], writes=[B_t])
    op("dve", lambda e: e.tensor_copy(out=pw(PWR, 1), in_=AR), reads=[B_t], writes=[B_t])
    op("dve", lambda e: e.tensor_copy(out=pw(PWI, 1), in_=AI), reads=[B_t], writes=[B_t])
    m_ = 1
    while m_ < 8:
        r3 = lambda a, lo, n: a[:, 16 * lo:16 * (lo + n)].rearrange("p (n q) -> p n q", n=n)
        br = pw(PWR, m_).unsqueeze(1).to_broadcast([128, m_, 16])
        bi = pw(PWI, m_).unsqueeze(1).to_broadcast([128, m_, 16])
        t1 = U1[:, 0:16 * m_].rearrange("p (n q) -> p n q", n=m_)
        t2 = U2[:, 0:16 * m_].rearrange("p (n q) -> p n q", n=m_)
        tt("dve", t1, r3(PWR, 1, m_), br, ALU.mult)
        tt("dve", t2, r3(PWI, 1, m_), bi, ALU.mult)
        tt("dve", r3(PWR, m_ + 1, m_), t1, t2, ALU.subtract)
        tt("dve", t1, r3(PWR, 1, m_), bi, ALU.mult)
        tt("dve", t2, r3(PWI, 1, m_), br, ALU.mult)
        tt("dve", r3(PWI, m_ + 1, m_), t1, t2, ALU.add)
        m_ *= 2
    for _ in range(3):
        csquare(CS, SN)
    E1 = xin[0][:]
    E2 = xin[1][:]
    E13 = E1.rearrange("p (q c) -> p q c", q=16)
    E23 = E2.rearrange("p (q c) -> p q c", q=16)

    def ctt(eng, out, a, b, o, wr_const=False):
        op(eng, lambda e: e.tensor_tensor(out=out, in0=a, in1=b, op=o), reads=[B_t, B_const],
           writes=[B_const if wr_const else B_t])

    op("dve", lambda e: e.tensor_copy(out=EC[:, :, 0:1], in_=CS.unsqueeze(2)), reads=[B_t], writes=[B_const])
    op("dve", lambda e: e.tensor_copy(out=ES[:, :, 0:1], in_=SN.unsqueeze(2)), reads=[B_t], writes=[B_const])
    n = 1
    while n < 64:
        cb = EC[:, :, n - 1:n].to_broadcast([128, 16, n])
        sb = ES[:, :, n - 1:n].to_broadcast([128, 16, n])
        ctt("dve", E13[:, :, 0:n], EC[:, :, 0:n], cb, ALU.mult)
        ctt("dve", E23[:, :, 0:n], ES[:, :, 0:n], sb, ALU.mult)
        ctt("dve", EC[:, :, n:2 * n], E13[:, :, 0:n], E23[:, :, 0:n], ALU.subtract, True)
        ctt("dve", E13[:, :, 0:n], EC[:, :, 0:n], sb, ALU.mult)
        ctt("dve", E23[:, :, 0:n], ES[:, :, 0:n], cb, ALU.mult)
        ctt("dve", ES[:, :, n:2 * n], E13[:, :, 0:n], E23[:, :, 0:n], ALU.add, True)
        n *= 2

    CTr = salloc(512)
    CTi = salloc(512)
    INr = salloc(512)
    INi = salloc(512)
    INr3 = INr.rearrange("p (T c) -> p T c", T=4)
    INi3 = INi.rearrange("p (T c) -> p T c", T=4)
    op("dve", lambda e: e.memset(INr, 0.0), reads=[B_t], writes=[B_s5, B_t])
    op("dve", lambda e: e.memset(INi, 0.0), reads=[B_t], writes=[B_s5, B_t])
    B_cin = Buf("cin")
    for s in range(4):
        for m in range(2):
            p0 = 32 * s + 16 * m
            for (src, dst) in ((c_re, INr3), (c_im, INi3)):
                op("sp", lambda e, s=s, m=m, p0=p0, src=src, dst=dst: e.dma_start(
                    out=dst[p0:p0 + 16, :, 64 * m:64 * m + 64],
                    in_=src.rearrange("(T s m) o p -> s m o T p", s=4, m=2)[s, m]),
                   reads=[B_s5], writes=[B_cin], dma=True)
    for (src3, dstv) in ((INr3, CTr), (INi3, CTi)):
        for T in range(4):
            bk = next_bank()
            op("pe", lambda e, src3=src3, T=T, bk=bk: e.transpose(bank(bk)[:, 0:128], src3[:, T, :], idf),
               reads=[B_cin, B_s5, B_id], writes=[b_bank[bk]])
            op("dve", lambda e, dstv=dstv, T=T, bk=bk: e.tensor_copy(out=dstv[:, 128 * T:128 * T + 128], in_=bank(bk)[:, 0:128]),
               reads=[b_bank[bk]], writes=[B_t])
    CTr3 = CTr.rearrange("p (q c) -> p q c", q=16)
    CTi3 = CTi.rearrange("p (q c) -> p q c", q=16)
    so["o"] -= 1024
    BmR = salloc(512)
    NBmI = salloc(512)
    BmR3 = BmR.rearrange("p (q c) -> p q c", q=16)
    NBmI3 = NBmI.rearrange("p (q c) -> p q c", q=16)
    op("dve", lambda e: e.memset(BmR, 0.0), reads=[B_t], writes=[B_t])
    op("dve", lambda e: e.memset(NBmI, 0.0), reads=[B_t], writes=[B_t])
    for m in range(2):
        sl = slice(64 * m, 64 * m + 64)
        op("dve", lambda e, m=m, sl=sl: e.tensor_copy(out=BmR3[sl, :, 16 * m:16 * m + 16], in_=v3(BBR)[sl]),
           reads=[B_t], writes=[B_t])
        op("dve", lambda e, m=m, sl=sl: e.tensor_scalar(out=NBmI3[sl, :, 16 * m:16 * m + 16], in0=v3(BBI)[sl], scalar1=-1.0,
                                                      scalar2=None, op0=ALU.mult, op1=ALU.bypass),
           reads=[B_t], writes=[B_t])
    M32 = xin[0][:, 512:640]
    op("dve", lambda e: e.memset(M32, 0.0), reads=[B_t], writes=[B_t])
    for s in range(4):
        op("dve", lambda e, s=s: e.memset(M32[32 * s:32 * s + 32, 32 * s:32 * s + 32], 1.0), reads=[B_t], writes=[B_t])
    vp_mark = so["o"]
    VPr = salloc(512)
    VPi = salloc(512)
    VT1 = xin[0][:, 0:512]
    VT2 = xin[1][:, 0:512]
    KT = xin[0][:, 640:768]
    DG = xin[0][:, 768:896]
    g3 = lambda a: a.rearrange("p (q c) -> p q c", q=16)
    B_t2 = Buf("s5tmp2")
    fork = list(B_t.writers[-1:])
    WmR = scr[:, 5348:5860]
    WmI = xin[1][:, 512:1024]

    def tt2(out, a, b, o):
        op("pool", lambda e: e.tensor_tensor(out=out, in0=a, in1=b, op=o), reads=[B_t2], writes=[B_t2], deps=fork)

    op("pool", lambda e: e.memset(WmR, 0.0), reads=[B_t2], writes=[B_t2], deps=fork)
    op("pool", lambda e: e.memset(WmI, 0.0), reads=[B_t2], writes=[B_t2], deps=fork)

    def w_iter(j):
        n = 7 - j
        pr = pw(PWR, n).unsqueeze(2).to_broadcast([128, 16, 16])
        pi = pw(PWI, n).unsqueeze(2).to_broadcast([128, 16, 16])
        tt2(v3(U1), v3(BBR), pr, ALU.mult)
        tt2(v3(U2), v3(BBI), pi, ALU.mult)
        for m in range(2):
            sl = slice(64 * m, 64 * m + 64)
            tt2(g3(WmR)[sl, :, 16 * m:16 * m + 16], v3(U1)[sl], v3(U2)[sl], ALU.subtract)
        tt2(v3(U1), v3(BBR), pi, ALU.mult)
        tt2(v3(U2), v3(BBI), pr, ALU.mult)
        for m in range(2):
            sl = slice(64 * m, 64 * m + 64)
            tt2(g3(WmI)[sl, :, 16 * m:16 * m + 16], v3(U1)[sl], v3(U2)[sl], ALU.add)
        for ri, Wm in ((0, WmR), (1, WmI)):
            for T in range(4):
                bk = next_bank()
                op("pe", lambda e, Wm=Wm, T=T, bk=bk: e.transpose(bank(bk)[:, 0:128], Wm[:, 128 * T:128 * T + 128], idf),
                   reads=[B_t2, B_id], writes=[b_bank[bk]])
                op("act", lambda e, T=T, j=j, ri=ri, bk=bk: e.activation(out=WS[:, T, j, ri, :], in_=bank(bk)[:, 0:128], func=AF.Copy),
                   reads=[b_bank[bk]], writes=[B_const])

    for n in range(0, 9):
        if n < 8:
            w_iter(n)
        pr = pw(PWR, n).unsqueeze(2).to_broadcast([128, 16, 32])
        pi = pw(PWI, n).unsqueeze(2).to_broadcast([128, 16, 32])
        tt("dve", g3(VT1), CTr3, pr, ALU.mult)
        tt("dve", g3(VT2), CTi3, pi, ALU.mult)
        tt("dve", g3(VPr), g3(VT1), g3(VT2), ALU.subtract)
        tt("dve", g3(VT1), CTr3, pi, ALU.mult)
        tt("dve", g3(VT2), CTi3, pr, ALU.mult)
        tt("dve", g3(VPi), g3(VT1), g3(VT2), ALU.add)
        if n >= 1:
            j = n - 1
            op("dve", lambda e, j=j: e.tensor_copy(out=VV[:, :, j, 0, :], in_=g3(VPr)), reads=[B_t], writes=[B_const])
            op("dve", lambda e, j=j: e.tensor_scalar(out=VV[:, :, j, 1, :], in0=g3(VPi), scalar1=-1.0, scalar2=None,
                                                   op0=ALU.mult, op1=ALU.bypass), reads=[B_t], writes=[B_const])
        if n <= 7:
            tau = n
            for T in range(4):
                bk = next_bank()
                op("pe", lambda e, T=T, bk=bk: e.matmul(bank(bk)[:, 0:128], lhsT=BmR[:, 128 * T:128 * T + 128],
                                                       rhs=VPr[:, 128 * T:128 * T + 128], start=True, stop=False),
                   reads=[B_t], writes=[b_bank[bk]])
                op("pe", lambda e, T=T, bk=bk: e.matmul(bank(bk)[:, 0:128], lhsT=NBmI[:, 128 * T:128 * T + 128],
                                                       rhs=VPi[:, 128 * T:128 * T + 128], start=False, stop=True),
                   reads=[B_t], writes=[b_bank[bk]])
                if tau == 0:
                    op("dve", lambda e, bk=bk: e.tensor_tensor(out=KT, in0=bank(bk)[:, 0:128], in1=M32, op=ALU.mult),
                       reads=[b_bank[bk], B_t], writes=[B_t])
                    op("dve", lambda e, T=T: e.tensor_scalar(out=DG, in0=idf, scalar1=DS[:, T:T + 1], scalar2=None,
                                                           op0=ALU.mult, op1=ALU.bypass), reads=[B_t, B_s5, B_id], writes=[B_t])
                    op("dve", lambda e, T=T: e.tensor_tensor(out=Kc[:, T, 0, :], in0=KT, in1=DG, op=ALU.add),
                       reads=[B_t], writes=[B_const])
                else:
                    op("dve", lambda e, T=T, bk=bk, tau=tau: e.tensor_tensor(out=Kc[:, T, tau, :], in0=bank(bk)[:, 0:128],
                                                                          in1=M32, op=ALU.mult),
                       reads=[b_bank[bk], B_t], writes=[B_const])
    P.barrier()
    if mode == "setup":
        dbg = {"WsT": (WsT, [128, 1024]), "EC": (EC, [128, 1024]), "ES": (ES, [128, 1024]),
               "RHO": (RHO, [128, 16]), "WS": (WS, [128, 8192]), "VV": (VV, [128, 8192]), "Kc": (Kc, [128, 4096]),
               "lnb_bc": (lnb_bc, [128, 1024])}
        sts = []
        for name, (t, shp) in dbg.items():
            o = dt("dbg_" + name, shp, t.dtype, kind="ExternalOutput")
            nd = len(t.shape)
            src = t[:] if nd == 2 else t[:].rearrange({3: "p a b -> p (a b)", 4: "p a b c -> p (a b c)", 5: "p a b c d -> p (a b c d)"}[nd])
            sts.append(op("sp", lambda e, o=o, src=src: e.dma_start(out=o, in_=src), reads=[B_const], dma=True, sembuf=Buf()))
        op("sp", None, deps=sts)
        P.finalize()
        P.close()
        return nc

    ld = {"next": 0, "done": -1}

    def load_block(gp):
        blk = SEQ[gp % NBLK]
        slot = gp % NR
        w = 1024 if BLK_GLU <= blk < BLK_WO else 2048
        op("sp", lambda e: e.dma_start(out=ring[slot][:, 0:w], in_=wsc[blk, :, 0:w]), reads=[b_wblk[blk]], writes=[b_ring[slot]], dma=True)

    def advance(cur):
        while ld["next"] < nst * NBLK and ld["next"] - NR <= ld["done"] and ld["next"] <= cur + NR - 1:
            load_block(ld["next"])
            ld["next"] += 1

    def need(gp):
        advance(gp)
        assert ld["next"] > gp, (gp, ld)

    defer = {"on": False, "q": []}

    def done(gp):
        if defer["on"]:
            defer["q"].append(gp)
            return
        assert gp == ld["done"] + 1, (gp, ld)
        ld["done"] = gp
        advance(gp + 1)

    statn = {"i": 0}

    def stat_slot():
        i = statn["i"] % 32
        statn["i"] += 1
        return stat[:, 2 * i:2 * i + 1], stat[:, 2 * i + 1:2 * i + 2]

    def x_rows(ap, base):
        return ap[base:base + 128, :].rearrange("(c j) d -> j c d", j=8)

    def rmsnorm_rstd(src, junk, b_src, b_junk, tag):
        ss, rs = stat_slot()
        b = Buf(tag)
        op("act", lambda e: e.activation(out=junk, in_=src, func=AF.Square, accum_out=ss), reads=[b_src], writes=[b_junk, b])
        op("pool", lambda e: e.tensor_scalar(out=rs, in0=ss, scalar1=1.0 / D, scalar2=EPS, op0=ALU.mult, op1=ALU.add),
           reads=[b], writes=[b])
        op("pool", lambda e: e.tensor_tensor(out=rs, in0=rs, in1=mhalf[:], op=ALU.pow), reads=[b, B_const], writes=[b])
        return rs, b

    def proj_tile(blk_base, ct, nk, rhs_of, rhs_bufs, g0):
        gp = g0 + POS[blk_base + ct // 2]
        need(gp)
        slot = gp % NR
        bk = next_bank()
        for k in range(nk):
            op("pe", lambda e, k=k, slot=slot, bk=bk: e.matmul(bank(bk), lhsT=ring[slot][:, 256 * k + 128 * (ct % 2):256 * k + 128 * (ct % 2) + 128],
                                                            rhs=rhs_of(k), start=(k == 0), stop=(k == nk - 1)),
               reads=[b_ring[slot]] + rhs_bufs(k), writes=[b_bank[bk]])
        if ct % 2 == 1:
            done(gp)
        return bk

    def finish():
        op("sp", None, deps=P._reduce(P.out_stores) if P.out_stores else [])
        P.finalize()
        P.close()
        return nc

    def prepA_steps(st):
        tok0 = st * ST
        rsb = {}

        def load(C):
            s = C % 2
            base = tok0 + 128 * C
            op("act", lambda e, s=s, base=base: e.dma_start(out=xin[s][:], in_=x_rows(x_d, base)), writes=[b_xin[s]], dma=True)

        def square(C):
            s = C % 2
            rsb[C] = rmsnorm_rstd(xin[s][:], xs[C][:], b_xin[s], b_xs[C], "rs")

        def scale(C):
            s = C % 2
            rs, brs = rsb[C]
            op("act", lambda e, s=s, C=C, rs=rs: e.activation(out=xs[C][:], in_=xin[s][:], func=AF.Identity, scale=rs),
               reads=[b_xin[s], brs], writes=[b_xs[C]])

        return [lambda: (load(0), load(1)), lambda: (square(0), square(1)), lambda: (scale(0), load(2), scale(1), load(3)),
                lambda: (square(2), square(3)), lambda: (scale(2), scale(3))]

    def prepA(st):
        for f in prepA_steps(st):
            f()

    def prepB(st, only=None):
        xnT = xnT2[st % 2]
        b_xnT = b_xnT2[st % 2]
        for C in (range(4) if only is None else [only]):
            for k in range(8):
                op("pe", lambda e, C=C, k=k: e.transpose(psT[:, 128 * k:128 * k + 128], xs[C][:, 128 * k:128 * k + 128], idb[:]),
                   reads=[b_xs[C], B_const], writes=[b_psT])
            op("dve", lambda e, C=C, xnT=xnT: e.tensor_tensor(out=xnT[:, :, 128 * C:128 * C + 128],
                                                             in0=psT[:].rearrange("p (k t) -> p k t", k=8),
                                                             in1=gain_t[:].unsqueeze(2).to_broadcast([128, 8, 128]), op=ALU.mult),
               reads=[b_psT, B_const], writes=[b_xnT])

    prepA(0)
    prepB(0)
    for st in range(nst):
        g0 = st * NBLK
        tok0 = st * ST
        xnT = xnT2[st % 2]
        b_xnT = b_xnT2[st % 2]
        if upto == 1:
            return finish()
        pa_steps = prepA_steps(st + 1) if st + 1 < nst else None
        if pa_steps:
            pa_steps[0]()
        xr = lambda k, xnT_=xnT: xnT_[:, k, :]
        xb_ = lambda k, b_=b_xnT: [b_]
        gpv = g0 + POS[BLK_V]
        need(gpv + 3)
        def u_tile(ct):
            bk = proj_tile(BLK_U, ct, 8, xr, xb_, g0)
            op("act", lambda e, ct=ct, bk=bk: e.activation(out=gusz[:, ct, :], in_=bank(bk), func=AF.Gelu_apprx_tanh),
               reads=[b_bank[bk]], writes=[b_gusz[ct]])
            if pa_steps and ct % 2 == 0:
                pa_steps[1 + ct // 2]()

        defer["on"] = True
        for C in range(4):
            pb = next_pair()
            s = C % 2
            for vb in range(4):
                slot = (gpv + vb) % NR
                for k in range(8):
                    op("pe", lambda e, C=C, vb=vb, k=k, slot=slot, pb=pb, xnT_=xnT: e.matmul(
                        psA[:, 512 * pb + 256 * vb:512 * pb + 256 * vb + 256], lhsT=xnT_[:, k, 128 * C:128 * C + 128],
                        rhs=ring[slot][:, 256 * k:256 * k + 256], start=(k == 0), stop=(k == 7)),
                       reads=[b_xnT, b_ring[slot]], writes=[b_bank[pb], b_bank[pb + 1]])
            op("act", lambda e, s=s, pb=pb: e.activation(out=gv[s], in_=psA[:, 512 * pb:512 * pb + 1024], func=AF.Gelu_apprx_tanh),
               reads=[b_bank[pb], b_bank[pb + 1]], writes=[b_gv[s]])
            i6 = statn["i"] % 4
            statn["i"] += 1
            bnst = P_bn[i6]
            mv = P_mv[i6]
            bmv = Buf("mv")
            op("dve", lambda e, s=s, bnst=bnst: e.bn_stats(out=bnst[:, 0, :], in_=gv[s][:, 0:512]), reads=[b_gv[s]], writes=[bmv])
            op("dve", lambda e, s=s, bnst=bnst: e.bn_stats(out=bnst[:, 1, :], in_=gv[s][:, 512:1024]), reads=[b_gv[s]], writes=[bmv])
            op("dve", lambda e, bnst=bnst, mv=mv: e.bn_aggr(out=mv[:, 0:2], in_=bnst[:].rearrange("p a b -> p (a b)")),
               reads=[bmv], writes=[bmv])
            op("pool", lambda e, mv=mv: e.tensor_scalar(out=mv[:, 2:3], in0=mv[:, 1:2], scalar1=EPS, scalar2=None,
                                                      op0=ALU.add, op1=ALU.bypass), reads=[bmv], writes=[bmv])
            op("pool", lambda e, mv=mv: e.tensor_tensor(out=mv[:, 2:3], in0=mv[:, 2:3], in1=mhalf[:], op=ALU.pow),
               reads=[bmv, B_const], writes=[bmv])
            op("dve", lambda e, s=s, mv=mv: e.tensor_scalar(out=gv[s], in0=gv[s], scalar1=mv[:, 0:1], scalar2=mv[:, 2:3],
                                                          op0=ALU.subtract, op1=ALU.mult), reads=[bmv, b_gv[s]], writes=[b_gv[s]])
            op("dve", lambda e, s=s: e.tensor_tensor(out=gv[s], in0=gv[s], in1=lng_bc[:], op=ALU.mult),
               reads=[b_gv[s], B_const], writes=[b_gv[s]])
            op("dve", lambda e, s=s, C=C: e.tensor_tensor(out=vn[:, C, :], in0=gv[s], in1=lnb_bc[:], op=ALU.add),
               reads=[b_gv[s], B_const], writes=[b_vn[C]])
            if C in (1, 2):
                rr["avoid"] = {pb, pb + 1, prev_pb, prev_pb + 1}
                u_tile(2 * C - 2)
                u_tile(2 * C - 1)
                rr["avoid"] = None
            prev_pb = pb
        defer["on"] = False
        for vb in range(4):
            done(gpv + vb)
        for gp_q in defer["q"]:
            done(gp_q)
        defer["q"] = []

        xr = lambda k, xnT_=xnT: xnT_[:, k, :]
        xb_ = lambda k, b_=b_xnT: [b_]
        for ct in range(4, 8):
            u_tile(ct)
        if upto == 2:
            return finish()
        for ct in range(8):
            bk = proj_tile(BLK_ZA, ct, 8, xr, xb_, g0)
            s = ct % 2
            op("act", lambda e, s=s, bk=bk: e.activation(out=szt[s][:], in_=bank(bk), func=AF.Silu),
               reads=[b_bank[bk]], writes=[b_szt[s]])
            op("dve", lambda e, s=s, ct=ct: e.tensor_tensor(out=gusz[:, ct, :], in0=gusz[:, ct, :], in1=szt[s][:], op=ALU.mult),
               reads=[b_szt[s], b_gusz[ct]], writes=[b_gusz[ct]])
        if upto == 3:
            return finish()
        for ct in range(4):
            bk = proj_tile(BLK_XB, ct, 8, xr, xb_, g0)
            op("dve", lambda e, ct=ct, bk=bk: e.tensor_copy(out=ub[:, ct, :].rearrange("p (j c t) -> p c j t", j=8, c=4),
                                                          in_=bank(bk).rearrange("p (c j t) -> p c j t", c=4, j=8)),
               reads=[b_bank[bk]], writes=[b_ub[ct]])
        if upto == 4:
            return finish()
        for h in range(8):
            bk = next_bank()
            op("pe", lambda e, h=h, bk=bk: e.matmul(bank(bk), lhsT=ones1[0:1, :], rhs=bsrow[0:1, h, :], start=True, stop=False,
                                                   skip_group_check=True),
               reads=[B_const], writes=[b_bank[bk]])
            for C in range(4):
                op("pe", lambda e, h=h, C=C, bk=bk: e.matmul(bank(bk)[:, 128 * C:128 * C + 128], lhsT=vn[:, C, 128 * h:128 * h + 128],
                                                            rhs=WsT[:, h, :], start=False, stop=(C == 3), skip_group_check=True),
                   reads=[b_vn[C], B_const], writes=[b_bank[bk]])
            op("dve", lambda e, h=h, bk=bk: e.tensor_tensor(out=gusz[:, h, :], in0=bank(bk), in1=gusz[:, h, :], op=ALU.mult),
               reads=[b_bank[bk], b_gusz[h]], writes=[b_gusz[h]])
        if upto == 5:
            return finish()
        if st % SEQ_ST == 0:
            for G in range(4):
                op("pool", lambda e, G=G: e.memset(carry[:, 4 * G:4 * G + 4, :], 0.0), writes=[b_carry[G]])
        sb_bufs = [b_bank[i] for i in range(4)]
        for G in range(4):
            for ri in range(2):
                for j in range(8):
                    for s in range(4):
                        c0 = 128 * G + 64 * ri
                        op("pe", lambda e, G=G, s=s, ri=ri, j=j, c0=c0: e.matmul(
                            bank(s)[:, c0:c0 + 64],
                            lhsT=WS[32 * s:32 * s + 32, G, j, ri, :],
                            rhs=ub[32 * s:32 * s + 32, G, 64 * j:64 * j + 64],
                            start=(j == 0), stop=(j == 7), tile_position=(32 * s, 0), skip_group_check=True),
                           reads=[b_ub[G], B_const], writes=[b_bank[s]])
        rr["b"] = 4
        rr["hi"] = 12
        for G in range(4):
            S4 = psA[:, 0:2048].rearrange("p (s g r c) -> p s g r c", s=4, g=4, r=2)[:, :, G, :, :]
            ecb = EC[:, 4 * G:4 * G + 4, :].unsqueeze(2).to_broadcast([128, 4, 2, 64])
            esb = ES[:, 4 * G:4 * G + 4, :].unsqueeze(2).to_broadcast([128, 4, 2, 64])
            v4 = lambda a: a.rearrange("p (s r c) -> p s r c", s=4, r=2)
            Ta, Tb, Gi, Hf = s5t
            bTa, bTb, bGi, bHf = b_s5t
            Gt = gt2[G % 2][:]
            bGt = b_gt2[G % 2]
            Ua, Ub = [t[:] for t in s5u]
            bUa, bUb = b_s5u
            op("dve", lambda e, S4=S4, ecb=ecb: e.tensor_tensor(out=v4(Ta), in0=S4, in1=ecb, op=ALU.mult),
               reads=sb_bufs + [B_const], writes=[bTa])
            op("dve", lambda e, S4=S4, esb=esb: e.tensor_tensor(out=v4(Tb), in0=S4, in1=esb, op=ALU.mult),
               reads=sb_bufs + [B_const], writes=[bTb])
            op("dve", lambda e: e.tensor_tensor(out=v4(Gi)[:, :, 0, :], in0=v4(Ta)[:, :, 0, :], in1=v4(Tb)[:, :, 1, :], op=ALU.add),
               reads=[bTa, bTb], writes=[bGi])
            op("dve", lambda e: e.tensor_tensor(out=v4(Gi)[:, :, 1, :], in0=v4(Ta)[:, :, 1, :], in1=v4(Tb)[:, :, 0, :], op=ALU.subtract),
               reads=[bTa, bTb], writes=[bGi])
            for s in range(4):
                q = 4 * G + s
                for ri in range(2):
                    op("dve", lambda e, s=s, ri=ri, q=q, Gt=Gt: e.tensor_tensor_scan(
                        out=v4(Gt)[:, s, ri, :], data0=RHO[:, q:q + 1].to_broadcast([128, 64]), data1=v4(Gi)[:, s, ri, :],
                        initial=carry[:, q, ri:ri + 1], op0=ALU.mult, op1=ALU.add),
                       reads=[bGi, B_const, b_carry[G]], writes=[bGt])
            op("pool", lambda e, ecb=ecb, Gt=Gt: e.tensor_tensor(out=v4(Ua), in0=v4(Gt), in1=ecb, op=ALU.mult),
               reads=[bGt, B_const], writes=[bUa])
            op("pool", lambda e, esb=esb, Gt=Gt: e.tensor_tensor(out=v4(Ub), in0=v4(Gt), in1=esb, op=ALU.mult),
               reads=[bGt, B_const], writes=[bUb])
            op("pool", lambda e: e.tensor_tensor(out=v4(Hf)[:, :, 0, :], in0=v4(Ua)[:, :, 0, :], in1=v4(Ub)[:, :, 1, :], op=ALU.subtract),
               reads=[bUa, bUb], writes=[bHf])
            op("pool", lambda e: e.tensor_tensor(out=v4(Hf)[:, :, 1, :], in0=v4(Ua)[:, :, 1, :], in1=v4(Ub)[:, :, 0, :], op=ALU.add),
               reads=[bUa, bUb], writes=[bHf])
            op("pool", lambda e, G=G: e.tensor_copy(out=hsh[:, 4 * G:4 * G + 4, :, 0:1], in_=carry[:, 4 * G:4 * G + 4, :].unsqueeze(3)),
               reads=[b_carry[G]], writes=[b_hsh[G]])
            op("pool", lambda e, G=G: e.tensor_copy(out=hsh[:, 4 * G:4 * G + 4, :, 1:64], in_=v4(Hf)[:, :, :, 0:63]),
               reads=[bHf], writes=[b_hsh[G]])
            op("pool", lambda e, G=G: e.tensor_copy(out=carry[:, 4 * G:4 * G + 4, :].unsqueeze(3), in_=v4(Hf)[:, :, :, 63:64]),
               reads=[bHf], writes=[b_carry[G]])
        if upto == 6:
            return finish()
        for ct in range(4):
            bk = proj_tile(BLK_ZB, ct, 8, xr, xb_, g0)
            op("act", lambda e, ct=ct, bk=bk: e.activation(out=szb[:, ct, :], in_=bank(bk), func=AF.Silu),
               reads=[b_bank[bk]], writes=[b_szb[ct]])
        for ct in range(8):
            bk = proj_tile(BLK_GA, ct, 8, xr, xb_, g0)
            op("act", lambda e, ct=ct, bk=bk: e.activation(out=tga[:, ct, :], in_=bank(bk), func=AF.Tanh, scale=0.5),
               reads=[b_bank[bk]], writes=[b_tga[ct]])
        for ct in range(8):
            bk = proj_tile(BLK_AD, ct, 8, lambda k: gusz[:, k, :], lambda k: [b_gusz[k]], g0)
            op("dve", lambda e, ct=ct, bk=bk: e.scalar_tensor_tensor(out=yag[:, ct, :], in0=tga[:, ct, :], scalar=1.0,
                                                                    in1=bank(bk), op0=ALU.add, op1=ALU.mult),
               reads=[b_tga[ct], b_bank[bk]], writes=[b_yag[ct]])
        for T in range(4):
            bk = next_bank()
            for tau in range(8):
                w = 16 * (8 - tau)
                op("pe", lambda e, T=T, tau=tau, w=w, bk=bk: e.matmul(
                    bank(bk)[:, 64 * tau:512],
                    lhsT=Kc[:, T, tau, :], rhs=ub[:, T, 0:4 * w],
                    start=(tau == 0), stop=False, skip_group_check=True),
                   reads=[b_ub[T], B_const], writes=[b_bank[bk]])
            for j in range(8):
                for ri in range(2):
                    for s in range(4):
                        q = 4 * T + s
                        last = (ri == 1)
                        op("pe", lambda e, T=T, s=s, q=q, j=j, ri=ri, bk=bk, last=last: e.matmul(
                            bank(bk)[32 * s:32 * s + 32, 64 * j:64 * j + 64],
                            lhsT=VV[:, q, j, ri, :], rhs=hsh[:, q, ri, :],
                            start=False, stop=last, tile_position=(0, 32 * s), skip_group_check=True),
                           reads=[b_hsh[T], B_const], writes=[b_bank[bk]])
            op("act", lambda e, T=T, bk=bk: e.activation(out=gy[:, T, :].rearrange("p (c j t) -> p c j t", c=4, j=8),
                                                         in_=bank(bk).rearrange("p (j c t) -> p c j t", j=8, c=4),
                                                         func=AF.Gelu_apprx_tanh),
               reads=[b_bank[bk]], writes=[b_gy[T]])
        if upto == 7:
            return finish()
        gpg = g0 + POS[BLK_GLU + 2]
        gyr = lambda k: gy[:, k, :]
        gyb = lambda k: [b_gy[k]]
        for ct in range(4):
            s = ct % 2
            gq = g0 + POS[BLK_GLU + 2 + ct // 2]
            gp_ = g0 + POS[BLK_GLU + ct // 2]
            need(gq)
            need(gp_)
            bk = next_bank()
            for k in range(4):
                op("pe", lambda e, k=k, ct=ct, gq=gq, bk=bk: e.matmul(
                    bank(bk), lhsT=ring[gq % NR][:, 256 * k + 128 * (ct % 2):256 * k + 128 * (ct % 2) + 128], rhs=gy[:, k, :],
                    start=(k == 0), stop=(k == 3)), reads=[b_ring[gq % NR], b_gy[k]], writes=[b_bank[bk]])
            op("act", lambda e, s=s, bk=bk: e.activation(out=th[s][:], in_=bank(bk), func=AF.Tanh, scale=0.5),
               reads=[b_bank[bk]], writes=[b_th[s]])
            bk2 = next_bank()
            for k in range(4):
                op("pe", lambda e, k=k, ct=ct, gp_=gp_, bk2=bk2: e.matmul(
                    bank(bk2), lhsT=ring[gp_ % NR][:, 256 * k + 128 * (ct % 2):256 * k + 128 * (ct % 2) + 128], rhs=gy[:, k, :],
                    start=(k == 0), stop=(k == 3)), reads=[b_ring[gp_ % NR], b_gy[k]], writes=[b_bank[bk2]])
            op("dve", lambda e, s=s, bk2=bk2: e.scalar_tensor_tensor(out=tq[s][:], in0=th[s][:], scalar=1.0, in1=bank(bk2),
                                                                   op0=ALU.add, op1=ALU.mult),
               reads=[b_th[s], b_bank[bk2]], writes=[b_tq[s]])
            op("pool", lambda e, s=s, ct=ct: e.tensor_tensor(out=ybp[:, ct, :], in0=tq[s][:], in1=szb[:, ct, :], op=ALU.mult),
               reads=[b_tq[s], b_szb[ct]], writes=[b_ybp[ct]])
            if ct % 2 == 1:
                done(gq)
                done(gp_)
        if upto == 8:
            return finish()
        def o_load(C):
            s = C % 2
            base = tok0 + 128 * C
            op("act", lambda e, s=s, base=base: e.dma_start(out=xres[s][:], in_=x_rows(x_d, base)), writes=[b_xres[s]], dma=True)

        o_load(0)
        o_load(1)
        gpb = g0 + POS[BLK_GB]
        gpbd = g0 + POS[BLK_BD]
        for half in range(2):
            for c4 in range(4):
                ct = 4 * half + c4
                bk = proj_tile(BLK_GB, ct, 8, xr, xb_, g0)
                op("act", lambda e, c4=c4, bk=bk: e.activation(out=tga[:, c4, :], in_=bank(bk), func=AF.Tanh, scale=0.5),
                   reads=[b_bank[bk]], writes=[b_tga[c4]])
            for c4 in range(4):
                ct = 4 * half + c4
                bk = proj_tile(BLK_BD, ct, 4, lambda k: ybp[:, k, :], lambda k: [b_ybp[k]], g0)
                s = ct % 2
                op("dve", lambda e, s=s, c4=c4, bk=bk: e.scalar_tensor_tensor(out=tyb[s][:], in0=tga[:, c4, :], scalar=1.0,
                                                                            in1=bank(bk), op0=ALU.add, op1=ALU.mult),
                   reads=[b_tga[c4], b_bank[bk]], writes=[b_tyb[s]])
                op("dve", lambda e, s=s, ct=ct: e.scalar_tensor_tensor(out=yag[:, ct, :], in0=tyb[s][:], scalar=0.5, in1=yag[:, ct, :],
                                                                        op0=ALU.mult, op1=ALU.add),
                   reads=[b_tyb[s], b_yag[ct]], writes=[b_yag[ct]])
        if upto == 9:
            return finish()
        gpo = g0 + POS[BLK_WO]
        need(gpo + 3)
        ors = {}

        def o_stage1(C):
            s = C % 2
            if st + 1 < nst:
                prepB(st + 1, only=C)
            pb = next_pair()
            for ob in range(4):
                slot = (gpo + ob) % NR
                for k in range(8):
                    op("pe", lambda e, C=C, ob=ob, k=k, slot=slot, pb=pb: e.matmul(
                        psA[:, 512 * pb + 256 * ob:512 * pb + 256 * ob + 256], lhsT=yag[:, k, 128 * C:128 * C + 128],
                        rhs=ring[slot][:, 256 * k:256 * k + 256], start=(k == 0), stop=(k == 7)),
                       reads=[b_yag[k], b_ring[slot]], writes=[b_bank[pb], b_bank[pb + 1]])
            op("dve", lambda e, s=s, pb=pb: e.scalar_tensor_tensor(out=xres[s][:], in0=psA[:, 512 * pb:512 * pb + 1024], scalar=0.5,
                                                                   in1=xres[s][:], op0=ALU.mult, op1=ALU.add),
               reads=[b_bank[pb], b_bank[pb + 1], b_xres[s]], writes=[b_xres[s]])
            ors[C] = rmsnorm_rstd(xres[s][:], xs[C][:], b_xres[s], b_xs[C], "rs2")

        def o_stage2(C):
            s = C % 2
            base = tok0 + 128 * C
            rs, brs = ors[C]
            op("dve", lambda e, s=s, rs=rs: e.scalar_tensor_tensor(out=ost[s], in0=xres[s][:], scalar=rs, in1=fg_bc[:],
                                                                   op0=ALU.mult, op1=ALU.mult),
               reads=[b_xres[s], brs, B_const], writes=[b_ost[s]])
            P.out_stores.append(op("pool", lambda e, s=s, base=base: e.dma_start(out=x_rows(y_d, base), in_=ost[s]),
                                   reads=[b_ost[s]], dma=True))

        o_stage1(0)
        o_stage2(0)
        o_load(2)
        o_stage1(1)
        o_stage2(1)
        o_load(3)
        o_stage1(2)
        o_stage2(2)
        o_stage1(3)
        o_stage2(3)
        for ob in range(4):
            done(gpo + ob)

    op("sp", None, deps=P._reduce(P.out_stores))
    P.finalize()
    P.close()
    return nc


_CACHE = {}


def kernel(**inputs):
    x = np.ascontiguousarray(np.asarray(inputs["x"], dtype=np.float32))
    B, S, Dm = x.shape
    xs = x.reshape(NCORES, TOK, Dm)
    sq = lambda k: np.ascontiguousarray(np.asarray(inputs[k], dtype=np.float32)[0])
    shared = {k: sq(k) for k in ("norm_gain", "w_in", "a_ln_gain", "a_ln_bias", "a_spatial", "a_spatial_bias", "w_a_down",
                                 "lambda_re", "lambda_im", "log_dt", "b_re", "b_im", "c_re", "c_im", "d_skip", "w_glu",
                                 "w_b_down", "w_out")}
    shared["final_gain"] = np.ascontiguousarray(np.asarray(inputs["final_gain"], dtype=np.float32))
    if "nc" not in _CACHE:
        _CACHE["nc"] = build_program()
    nc = _CACHE["nc"]
    in_maps = [dict(shared, x=np.ascontiguousarray(xs[i])) for i in range(NCORES)]
    res = run_bass_kernel_spmd(nc, in_maps, core_ids=list(range(NCORES)))
    out = np.stack([np.asarray(r["y"]) for r in res.results], axis=0)
    return out.reshape(B, S, Dm).astype(np.float32)
```

```python
import numpy as np
from contextlib import ExitStack
import concourse.bass as bass
import concourse.mybir as mybir
from concourse.bass_utils import run_bass_kernel_spmd

F32 = mybir.dt.float32
BF16 = mybir.dt.bfloat16
AF = mybir.ActivationFunctionType
ALU = mybir.AluOpType

NCORES = 8
D = 1024
TOK = 8192
ST = 512
NST = TOK // ST
EPS = 1e-6
NR = 11
SEQ_ST = 2048 // ST


class Buf:
    def __init__(self, name=""):
        self.name = name
        self.writers = []
        self.readers = []
        self.phase_deps = []
        self.sem = None
        self.dma_cnt = 0


class Ins:
    __slots__ = ("eng", "fn", "deps", "signal", "idx", "is_dma", "sembuf", "semval", "cnt")

    def __init__(self, eng, fn, idx, is_dma=False):
        self.eng = eng
        self.fn = fn
        self.deps = []
        self.signal = False
        self.idx = idx
        self.is_dma = is_dma
        self.sembuf = None
        self.semval = 0
        self.cnt = 0


ENGS = ("pe", "act", "dve", "pool", "sp")


class Prog:
    def __init__(self, nc):
        self.nc = nc
        self.streams = {e: [] for e in ENGS}
        self.n = 0
        self.stack = ExitStack()
        self.dmas_since_barrier = []
        self.out_stores = []

    def sbuf(self, name, shape, dtype):
        return self.stack.enter_context(self.nc.sbuf_tensor(name, list(shape), dtype))

    def psum(self, name, shape, dtype=F32):
        return self.stack.enter_context(self.nc.psum_tensor(name, list(shape), dtype))

    def _reduce(self, deps):
        best = {}
        for d in deps:
            key = ("dma", id(d.sembuf)) if d.is_dma else d.eng
            if key not in best or best[key].idx < d.idx:
                best[key] = d
        return list(best.values())

    def op(self, eng, fn, reads=(), writes=(), dma=False, sembuf=None, deps=()):
        ins = Ins(eng, fn, self.n, is_dma=dma)
        self.n += 1
        dl = list(deps)
        for b in reads:
            dl.extend(b.writers)
        for b in writes:
            if b.readers:
                b.phase_deps = self._reduce(b.readers + b.writers)
                b.readers = []
                b.writers = []
            dl.extend(b.phase_deps)
        dl = [d for d in self._reduce(dl) if d is not ins]
        for d in dl:
            d.signal = True
        ins.deps = dl
        for b in reads:
            b.readers.append(ins)
        for b in writes:
            b.writers.append(ins)
        if dma:
            sb = sembuf if sembuf is not None else (writes[0] if writes else reads[0])
            ins.sembuf = sb
            sb.dma_cnt += 1
            ins.semval = 16 * sb.dma_cnt
            self.dmas_since_barrier.append(ins)
        self.streams[eng].append(ins)
        return ins

    def barrier(self):
        last = []
        for e in ENGS:
            for ins in reversed(self.streams[e]):
                if not ins.is_dma and ins.fn is not None:
                    last.append(ins)
                    break
        deps = self._reduce(last + self.dmas_since_barrier)
        self.dmas_since_barrier = []
        for e in ENGS:
            self.op(e, None, deps=deps)

    def finalize(self):
        nc = self.nc
        st = self.stack
        engsem = {e: st.enter_context(nc.semaphore("s_" + e)) for e in ENGS}
        for e in ENGS:
            for ins in self.streams[e]:
                if ins.is_dma and ins.sembuf.sem is None:
                    ins.sembuf.sem = st.enter_context(nc.semaphore("d%d" % ins.idx))
        for e in ENGS:
            c = 0
            for ins in self.streams[e]:
                if ins.is_dma or ins.fn is None:
                    continue
                if ins.signal:
                    c += 1
                    ins.cnt = c
        block = st.enter_context(nc.Block())

        def run(engname):
            def body(eng):
                waited = {}
                for ins in self.streams[engname]:
                    for d in ins.deps:
                        if d.is_dma:
                            sem, val = d.sembuf.sem, d.semval
                        else:
                            sem, val = engsem[d.eng], d.cnt
                        k = id(sem)
                        if waited.get(k, 0) >= val:
                            continue
                        waited[k] = val
                        eng.wait_ge(sem, val)
                    if ins.fn is None:
                        continue
                    r = ins.fn(eng)
                    if ins.is_dma:
                        r.then_inc(ins.sembuf.sem, 16)
                    elif ins.signal:
                        r.then_inc(engsem[engname], 1)
            return body

        block.tensor(run("pe"))
        block.scalar(run("act"))
        block.vector(run("dve"))
        block.gpsimd(run("pool"))
        block.sync(run("sp"))

    def close(self):
        self.stack.close()


BLK_U, BLK_V, BLK_ZA, BLK_XB, BLK_ZB, BLK_GA, BLK_GB = 0, 4, 8, 12, 14, 16, 20
BLK_AD, BLK_GLU, BLK_BD, BLK_WO = 24, 28, 32, 36
NBLK = 40
SEQ = (list(range(BLK_V, BLK_V + 4)) + list(range(BLK_U, BLK_U + 4)) + list(range(BLK_ZA, BLK_ZA + 4))
       + [BLK_XB, BLK_XB + 1, BLK_ZB, BLK_ZB + 1]
       + [BLK_GA, BLK_GA + 1, BLK_GA + 2, BLK_GA + 3, BLK_AD, BLK_AD + 1, BLK_AD + 2, BLK_AD + 3]
       + [BLK_GLU + 2, BLK_GLU, BLK_GLU + 3, BLK_GLU + 1]
       + [BLK_GB, BLK_GB + 1, BLK_BD, BLK_BD + 1, BLK_GB + 2, BLK_GB + 3, BLK_BD + 2, BLK_BD + 3]
       + list(range(BLK_WO, BLK_WO + 4)))
assert len(SEQ) == NBLK and sorted(SEQ) == list(range(NBLK))
POS = {b: i for i, b in enumerate(SEQ)}


def build_program(mode="full", nst=NST, upto=99):
    nc = bass.Bass("TRN2", target_bir_lowering=False)
    dt = lambda name, shape, dtype=F32, kind="ExternalInput": nc.dram_tensor(name, list(shape), dtype, kind=kind).ap()
    x_d = dt("x", [TOK, D])
    y_d = dt("y", [TOK, D], kind="ExternalOutput")
    norm_gain = dt("norm_gain", [D])
    w_in = dt("w_in", [D, 6144])
    a_ln_gain = dt("a_ln_gain", [D])
    a_ln_bias = dt("a_ln_bias", [D])
    a_spatial = dt("a_spatial", [8, 128, 128])
    a_sbias = dt("a_spatial_bias", [8, 128])
    w_a_down = dt("w_a_down", [D, D])
    lam_re = dt("lambda_re", [32, 64])
    lam_im = dt("lambda_im", [32, 64])
    log_dt = dt("log_dt", [32])
    b_re = dt("b_re", [32, 64, 16])
    b_im = dt("b_im", [32, 64, 16])
    c_re = dt("c_re", [32, 16, 64])
    c_im = dt("c_im", [32, 16, 64])
    d_skip = dt("d_skip", [512])
    w_glu = dt("w_glu", [512, 1024])
    w_b_down = dt("w_b_down", [512, D])
    w_out = dt("w_out", [D, D])
    final_gain = dt("final_gain", [D])
    wsc = dt("wsc", [NBLK, 128, 2048], BF16, kind=("Internal" if mode == "full" else "ExternalOutput"))

    P = Prog(nc)
    op = P.op

    idb = P.sbuf("idb", [128, 128], BF16)
    WsT = P.sbuf("WsT", [128, 8, 128], BF16)
    lng_bc = P.sbuf("lng_bc", [128, D], F32)
    lnb_bc = P.sbuf("lnb_bc", [128, D], BF16)
    fg_bc = P.sbuf("fg_bc", [128, D], F32)
    EC = P.sbuf("EC", [128, 16, 64], F32)
    ES = P.sbuf("ES", [128, 16, 64], F32)
    RHO = P.sbuf("RHO", [128, 16], F32)
    WS = P.sbuf("WS", [128, 4, 8, 2, 128], BF16)
    VV = P.sbuf("VV", [128, 16, 8, 2, 32], BF16)
    Kc = P.sbuf("Kc", [128, 4, 8, 128], BF16)
    mhalf = P.sbuf("mhalf", [128, 1], F32)
    gain_t = P.sbuf("gain_t", [128, 8], F32)
    bsrow = P.sbuf("bsrow", [1, 8, 512], BF16)
    ones1 = P.sbuf("ones1", [1, 128], BF16)
    B_const = Buf("const")

    xin = [P.sbuf("xin%d" % i, [128, D], F32) for i in range(2)]
    b_xin = [Buf("xin%d" % i) for i in range(2)]
    xres = xin
    b_xres = b_xin
    xs = [P.sbuf("xs%d" % i, [128, D], BF16) for i in range(4)]
    b_xs = [Buf() for _ in range(4)]
    stat = P.sbuf("stat", [128, 64], F32)
    P_bn = [P.sbuf("bn%d" % i, [128, 2, 6], F32) for i in range(4)]
    P_mv = [P.sbuf("mv%d" % i, [128, 4], F32) for i in range(4)]
    xnT2 = [P.sbuf("xnT0", [128, 8, ST], BF16)] * 2
    b_xnT2 = [Buf("xnT0")] * 2
    gusz = P.sbuf("gusz", [128, 8, ST], BF16)
    b_gusz = [Buf() for _ in range(8)]
    vn = P.sbuf("vn", [128, 4, D], BF16)
    b_vn = [Buf() for _ in range(4)]
    ub = P.sbuf("ub", [128, 4, ST], BF16)
    b_ub = [Buf() for _ in range(4)]
    szb = P.sbuf("szb", [128, 4, ST], BF16)
    b_szb = [Buf() for _ in range(4)]
    tga = P.sbuf("tga", [128, 8, ST], BF16)
    b_tga = [Buf() for _ in range(8)]
    hsh = P.sbuf("hsh", [128, 16, 2, 64], BF16)
    b_hsh = [Buf() for _ in range(4)]
    carry = P.sbuf("carry", [128, 16, 2], F32)
    b_carry = [Buf() for _ in range(4)]
    gy = ub
    b_gy = b_ub
    th = [P.sbuf("th%d" % i, [128, ST], BF16) for i in range(2)]
    b_th = [Buf() for _ in range(2)]
    szt = th
    b_szt = b_th
    tq = [P.sbuf("tq%d" % i, [128, ST], BF16) for i in range(2)]
    b_tq = [Buf() for _ in range(2)]
    ybp = szb
    b_ybp = b_szb
    yag = vn[:].rearrange("p c (h t) -> p (c h) t", h=2)
    b_yag = [b_vn[i // 2] for i in range(8)]
    tyb = tq
    b_tyb = b_tq
    s5u = [P.sbuf("s5u%d" % i, [128, 512], F32) for i in range(2)]
    b_s5u = [Buf() for _ in range(2)]
    gt2 = [P.sbuf("gt2_0", [128, 512], F32)] * 2
    b_gt2 = [Buf()] * 2
    ring = [P.sbuf("ring%d" % i, [128, 2048], BF16) for i in range(NR)]
    b_ring = [Buf("ring%d" % i) for i in range(NR)]
    SCR_N = 6144
    scr = P.sbuf("scr", [128, SCR_N], F32)
    gv = [scr[:, 0:1024], scr[:, 5120:6144]]
    b_gv = [Buf() for _ in range(2)]
    s5t = [scr[:, 1024 + 512 * i:1536 + 512 * i] for i in range(4)]
    b_s5t = [Buf() for _ in range(4)]
    ost = [scr[:, 3072:4096], scr[:, 4096:5120]]
    b_ost = [Buf() for _ in range(2)]
    psA = P.psum("psA", [128, 7 * 512], F32)
    psT = P.psum("psT", [128, 1024], BF16)
    b_bank = [Buf("bank%d" % i) for i in range(7)]
    b_psT = Buf("psT")
    bank = lambda i: psA[:, 512 * i:512 * (i + 1)]
    rr = {"b": 0, "p": 0}

    def next_bank():
        if rr.get("avoid"):
            cand = [b for b in range(7) if b not in rr["avoid"]]
            i = cand[rr["b"] % len(cand)]
            rr["b"] += 1
            return i
        if rr.get("hi", 0) > 0:
            rr["hi"] -= 1
            i = 4 + rr["b"] % 3
            rr["b"] += 1
            return i
        i = rr["b"] % 7
        rr["b"] += 1
        return i

    def next_pair():
        i = (rr["p"] % 3) * 2
        rr["p"] += 1
        return i

    so = {"o": 0}

    def salloc(n):
        o = so["o"]
        so["o"] += n
        assert so["o"] <= SCR_N
        return scr[:, o:o + n]

    B_s5 = Buf("s5in")
    DS, LR, LI, LDT = scr[:, 6140:6144], scr[:, 6124:6140], scr[:, 6108:6124], scr[:, 6092:6108]
    BR, BI = scr[:, 5836:6092], scr[:, 5580:5836]
    for m in range(2):
        sl = slice(64 * m, 64 * m + 64)
        op("act", lambda e, m=m, sl=sl: e.dma_start(out=LR[sl, :], in_=lam_re.rearrange("(q m) p -> m p q", m=2)[m],
                                                  allow_slow_non_contiguous=True), writes=[B_s5], dma=True)
        op("act", lambda e, m=m, sl=sl: e.dma_start(out=LI[sl, :], in_=lam_im.rearrange("(q m) p -> m p q", m=2)[m],
                                                  allow_slow_non_contiguous=True), writes=[B_s5], dma=True)
        op("act", lambda e, m=m, sl=sl: e.dma_start(out=LDT[sl, :],
                                                  in_=log_dt.rearrange("(q m) -> m q", m=2)[m:m + 1].to_broadcast([64, 16]),
                                                  allow_slow_non_contiguous=True), writes=[B_s5], dma=True)
    BR3 = BR.rearrange("p (q h) -> p q h", q=16)
    BI3 = BI.rearrange("p (q h) -> p q h", q=16)
    for m in range(2):
        sl = slice(64 * m, 64 * m + 64)
        op("act", lambda e, m=m, sl=sl: e.dma_start(out=BR3[sl], in_=b_re.rearrange("(q m) p h -> m p q h", m=2)[m]),
           writes=[B_s5], dma=True)
        op("act", lambda e, m=m, sl=sl: e.dma_start(out=BI3[sl], in_=b_im.rearrange("(q m) p h -> m p q h", m=2)[m]),
           writes=[B_s5], dma=True)
    op("act", lambda e: e.dma_start(out=DS, in_=d_skip.rearrange("(T p) -> p T", p=128), allow_slow_non_contiguous=True),
       writes=[B_s5], dma=True)

    idf = salloc(128)
    B_id = Buf("id")
    op("pool", lambda e: e.memset(idf, 1.0), writes=[B_id])
    op("pool", lambda e: e.affine_select(out=idf, in_=idf, pattern=[[1, 128]], compare_op=ALU.is_equal,
                                         fill=0.0, base=0, channel_multiplier=-1), reads=[B_id], writes=[B_id])
    op("dve", lambda e: e.tensor_copy(out=idb[:], in_=idf), reads=[B_id], writes=[B_const])
    op("dve", lambda e: e.memset(mhalf[:], -0.5), writes=[B_const])

    B_bc = Buf("bc")
    lnb32 = salloc(1024)
    bs32 = salloc(1024)
    op("sp", lambda e: e.dma_start(out=lng_bc[:], in_=a_ln_gain.unsqueeze(0).to_broadcast([128, D])), writes=[B_bc], dma=True)
    op("sp", lambda e: e.dma_start(out=fg_bc[:], in_=final_gain.unsqueeze(0).to_broadcast([128, D])), writes=[B_bc], dma=True)
    op("sp", lambda e: e.dma_start(out=lnb32, in_=a_ln_bias.unsqueeze(0).to_broadcast([128, D])), writes=[B_bc], dma=True)
    op("sp", lambda e: e.dma_start(out=bs32, in_=a_sbias.rearrange("h t -> (h t)").unsqueeze(0).to_broadcast([128, D])),
       writes=[B_bc], dma=True)
    op("dve", lambda e: e.tensor_copy(out=lnb_bc[:], in_=lnb32), reads=[B_bc], writes=[B_const])
    for h in range(8):
        for C in range(4):
            op("dve", lambda e, h=h, C=C: e.tensor_copy(
                out=bsrow[0:1, h, 128 * C:128 * C + 128].rearrange("p (j c) -> p j c", j=8),
                in_=bs32[0:1, 128 * h:128 * h + 128].rearrange("p (c j) -> p j c", j=8)), reads=[B_bc], writes=[B_const])
    op("dve", lambda e: e.memset(ones1[:], 1.0), writes=[B_const])

    Wn = salloc(1024)
    Wp = salloc(1024)
    B_w = Buf("wn")
    Wn3 = Wn.rearrange("p (h s) -> p h s", h=8)
    Wp3 = Wp.rearrange("p (h s) -> p h s", h=8)
    op("sp", lambda e: e.dma_start(out=Wn3, in_=a_spatial.rearrange("h t s -> t h s")), writes=[B_w], dma=True)
    op("pool", lambda e: e.affine_select(out=Wn3, in_=Wn3, pattern=[[0, 8], [-1, 128]], compare_op=ALU.is_ge,
                                         fill=0.0, base=0, channel_multiplier=1), reads=[B_w], writes=[B_w])
    B_wp = Buf("wp")
    for h in range(8):
        op("dve", lambda e, h=h: e.tensor_copy(out=Wp3[:, h, :].rearrange("p (j c) -> p j c", j=8),
                                               in_=Wn3[:, h, :].rearrange("p (c j) -> p j c", j=8)),
           reads=[B_w], writes=[B_wp])
    for h in range(8):
        bk = next_bank()
        op("pe", lambda e, h=h, bk=bk: e.transpose(bank(bk)[:, 0:128], Wp3[:, h, :], idf), reads=[B_wp, B_id], writes=[b_bank[bk]])
        op("dve", lambda e, h=h, bk=bk: e.tensor_copy(out=WsT[:, h, :].rearrange("p (j c) -> p j c", j=8),
                                                      in_=bank(bk)[:, 0:128].rearrange("p (c j) -> p j c", j=8)),
           reads=[b_bank[bk]], writes=[B_const])

    P.barrier()
    so["o"] = 128

    op("sp", lambda e: e.dma_start(out=gain_t[:], in_=norm_gain.rearrange("(k p) -> p k", p=128), allow_slow_non_contiguous=True),
       writes=[B_const], dma=True, sembuf=B_s5)
    b_wblk = [Buf("wblk%d" % i) for i in range(NBLK)]
    for blk in SEQ:
        if blk < BLK_AD:
            src = w_in.rearrange("(k p) c -> p k c", p=128)[:, :, 256 * blk:256 * blk + 256]
            nk = 8
        elif blk < BLK_GLU:
            c0 = 256 * (blk - BLK_AD)
            src = w_a_down.rearrange("(k p) c -> p k c", p=128)[:, :, c0:c0 + 256]
            nk = 8
        elif blk < BLK_BD:
            c0 = 256 * (blk - BLK_GLU)
            src = w_glu.rearrange("(k p) c -> p k c", p=128)[:, :, c0:c0 + 256]
            nk = 4
        elif blk < BLK_WO:
            c0 = 256 * (blk - BLK_BD)
            src = w_b_down.rearrange("(k p) c -> p k c", p=128)[:, :, c0:c0 + 256]
            nk = 4
        else:
            c0 = 256 * (blk - BLK_WO)
            src = w_out.rearrange("(k p) c -> p k c", p=128)[:, :, c0:c0 + 256]
            nk = 8
        w = nk * 256
        ins_ = op("pool", lambda e, blk=blk, w=w, nk=nk, src=src: e.dma_start(
            out=wsc[blk, :, 0:w].rearrange("p (k c) -> p k c", k=nk), in_=src), writes=[b_wblk[blk]], dma=True)
        P.dmas_since_barrier.remove(ins_)

    P.barrier()
    so["o"] = 128

    def t16(n=16):
        return salloc(n)
    DT, X1, MAG, TH = [t16() for _ in range(4)]
    B_t = Buf("s5tmp")

    def tt(eng, out, a, b, o):
        op(eng, lambda e: e.tensor_tensor(out=out, in0=a, in1=b, op=o), reads=[B_s5, B_t], writes=[B_t])

    def act(out, in_, func, scale=1.0):
        op("act", lambda e: e.activation(out=out, in_=in_, func=func, scale=scale), reads=[B_s5, B_t], writes=[B_t])

    def tsc(out, a, s1, s2, o0, o1):
        op("dve", lambda e: e.tensor_scalar(out=out, in0=a, scalar1=s1, scalar2=s2, op0=o0, op1=o1),
           reads=[B_s5, B_t], writes=[B_t])

    act(DT, LDT, AF.Exp)
    tt("dve", X1, LR, DT, ALU.mult)
    act(MAG, X1, AF.Exp)
    op("act", lambda e: e.activation(out=RHO[:], in_=X1, func=AF.Exp, scale=8.0), reads=[B_t], writes=[B_const])
    tt("dve", TH, LI, DT, ALU.mult)
    SN, CS, SH, T1, T2, T3 = [t16() for _ in range(6)]
    act(SN, TH, AF.Sin, 1.0 / 16.0)
    act(SH, TH, AF.Sin, 1.0 / 32.0)
    tt("dve", T1, SH, SH, ALU.mult)
    tsc(CS, T1, -2.0, 1.0, ALU.mult, ALU.add)

    def csquare(c, s):
        tt("dve", T1, c, c, ALU.mult)
        tt("dve", T2, s, s, ALU.mult)
        tt("dve", T3, c, s, ALU.mult)
        tt("dve", c, T1, T2, ALU.subtract)
        tsc(s, T3, 2.0, None, ALU.mult, ALU.bypass)

    for _ in range(4):
        csquare(CS, SN)
    AR, AI = t16(), t16()
    tt("dve", AR, MAG, CS, ALU.mult)
    tt("dve", AI, MAG, SN, ALU.mult)
    NRr, DEN, RDEN, KR, KI = [t16() for _ in range(5)]
    tsc(NRr, AR, -1.0, None, ALU.add, ALU.bypass)
    tt("dve", T1, LR, LR, ALU.mult)
    tt("dve", T2, LI, LI, ALU.mult)
    tt("dve", DEN, T1, T2, ALU.add)
    op("dve", lambda e: e.reciprocal(out=RDEN, in_=DEN), reads=[B_t], writes=[B_t])
    tt("dve", T1, NRr, LR, ALU.mult)
    tt("dve", T2, AI, LI, ALU.mult)
    tt("dve", T3, T1, T2, ALU.add)
    tt("dve", KR, T3, RDEN, ALU.mult)
    tt("dve", T1, AI, LR, ALU.mult)
    tt("dve", T2, NRr, LI, ALU.mult)
    tt("dve", T3, T1, T2, ALU.subtract)
    tt("dve", KI, T3, RDEN, ALU.mult)
    BBR, BBI, U1, U2 = [salloc(256) for _ in range(4)]
    v3 = lambda a: a.rearrange("p (q h) -> p q h", q=16)
    bq = lambda a, n=16: a.unsqueeze(2).to_broadcast([128, 16, n])
    tt("dve", v3(U1), BR3, bq(KR), ALU.mult)
    tt("dve", v3(U2), BI3, bq(KI), ALU.mult)
    tt("dve", v3(BBR), v3(U1), v3(U2), ALU.subtract)
    tt("dve", v3(U1), BI3, bq(KR), ALU.mult)
    tt("dve", v3(U2), BR3, bq(KI), ALU.mult)
    tt("dve", v3(BBI), v3(U1), v3(U2), ALU.add)
    PWR = salloc(144)
    PWI = salloc(144)
    pw = lambda a, n: a[:, 16 * n:16 * n + 16]
    op("dve", lambda e: e.memset(pw(PWR, 0), 1.0), reads=[B_t], writes=[B_t])
    op("dve", lambda e: e.memset(pw(PWI, 0), 0.0), reads=[B_t], writes=[B_t])
    op("dve", lambda e: e.tensor_copy(out=pw(PWR, 1), in_=AR), reads=[B_t], writes=[B_t])
    op("dve", lambda e: e.tensor_copy(out=pw(PWI, 1), in_=AI), reads=[B_t], writes=[B_t])
    m_ = 1
    while m_ < 8:
        r3 = lambda a, lo, n: a[:, 16 * lo:16 * (lo + n)].rearrange("p (n q) -> p n q", n=n)
        br = pw(PWR, m_).unsqueeze(1).to_broadcast([128, m_, 16])
        bi = pw(PWI, m_).unsqueeze(1).to_broadcast([128, m_, 16])
        t1 = U1[:, 0:16 * m_].rearrange("p (n q) -> p n q", n=m_)
        t2 = U2[:, 0:16 * m_].rearrange("p (n q) -> p n q", n=m_)
        tt("dve", t1, r3(PWR, 1, m_), br, ALU.mult)
        tt("dve", t2, r3(PWI, 1, m_), bi, ALU.mult)
        tt("dve", r3(PWR, m_ + 1, m_), t1, t2, ALU.subtract)
        tt("dve", t1, r3(PWR, 1, m_), bi, ALU.mult)
        tt("dve", t2, r3(PWI, 1, m_), br, ALU.mult)
        tt("dve", r3(PWI, m_ + 1, m_), t1, t2, ALU.add)
        m_ *= 2
    for _ in range(3):
        csquare(CS, SN)
    E1 = xin[0][:]
    E2 = xin[1][:]
    E13 = E1.rearrange("p (q c) -> p q c", q=16)
    E23 = E2.rearrange("p (q c) -> p q c", q=16)

    def ctt(eng, out, a, b, o, wr_const=False):
        op(eng, lambda e: e.tensor_tensor(out=out, in0=a, in1=b, op=o), reads=[B_t, B_const],
           writes=[B_const if wr_const else B_t])

    op("dve", lambda e: e.tensor_copy(out=EC[:, :, 0:1], in_=CS.unsqueeze(2)), reads=[B_t], writes=[B_const])
    op("dve", lambda e: e.tensor_copy(out=ES[:, :, 0:1], in_=SN.unsqueeze(2)), reads=[B_t], writes=[B_const])
    n = 1
    while n < 64:
        cb = EC[:, :, n - 1:n].to_broadcast([128, 16, n])
        sb = ES[:, :, n - 1:n].to_broadcast([128, 16, n])
        ctt("dve", E13[:, :, 0:n], EC[:, :, 0:n], cb, ALU.mult)
        ctt("dve", E23[:, :, 0:n], ES[:, :, 0:n], sb, ALU.mult)
        ctt("dve", EC[:, :, n:2 * n], E13[:, :, 0:n], E23[:, :, 0:n], ALU.subtract, True)
        ctt("dve", E13[:, :, 0:n], EC[:, :, 0:n], sb, ALU.mult)
        ctt("dve", E23[:, :, 0:n], ES[:, :, 0:n], cb, ALU.mult)
        ctt("dve", ES[:, :, n:2 * n], E13[:, :, 0:n], E23[:, :, 0:n], ALU.add, True)
        n *= 2

    CTr = salloc(512)
    CTi = salloc(512)
    INr = salloc(512)
    INi = salloc(512)
    INr3 = INr.rearrange("p (T c) -> p T c", T=4)
    INi3 = INi.rearrange("p (T c) -> p T c", T=4)
    op("dve", lambda e: e.memset(INr, 0.0), reads=[B_t], writes=[B_s5, B_t])
    op("dve", lambda e: e.memset(INi, 0.0), reads=[B_t], writes=[B_s5, B_t])
    B_cin = Buf("cin")
    for s in range(4):
        for m in range(2):
            p0 = 32 * s + 16 * m
            for (src, dst) in ((c_re, INr3), (c_im, INi3)):
                op("sp", lambda e, s=s, m=m, p0=p0, src=src, dst=dst: e.dma_start(
                    out=dst[p0:p0 + 16, :, 64 * m:64 * m + 64],
                    in_=src.rearrange("(T s m) o p -> s m o T p", s=4, m=2)[s, m]),
                   reads=[B_s5], writes=[B_cin], dma=True)
    for (src3, dstv) in ((INr3, CTr), (INi3, CTi)):
        for T in range(4):
            bk = next_bank()
            op("pe", lambda e, src3=src3, T=T, bk=bk: e.transpose(bank(bk)[:, 0:128], src3[:, T, :], idf),
               reads=[B_cin, B_s5, B_id], writes=[b_bank[bk]])
            op("dve", lambda e, dstv=dstv, T=T, bk=bk: e.tensor_copy(out=dstv[:, 128 * T:128 * T + 128], in_=bank(bk)[:, 0:128]),
               reads=[b_bank[bk]], writes=[B_t])
    CTr3 = CTr.rearrange("p (q c) -> p q c", q=16)
    CTi3 = CTi.rearrange("p (q c) -> p q c", q=16)
    so["o"] -= 1024
    BmR = salloc(512)
    NBmI = salloc(512)
    BmR3 = BmR.rearrange("p (q c) -> p q c", q=16)
    NBmI3 = NBmI.rearrange("p (q c) -> p q c", q=16)
    op("dve", lambda e: e.memset(BmR, 0.0), reads=[B_t], writes=[B_t])
    op("dve", lambda e: e.memset(NBmI, 0.0), reads=[B_t], writes=[B_t])
    for m in range(2):
        sl = slice(64 * m, 64 * m + 64)
        op("dve", lambda e, m=m, sl=sl: e.tensor_copy(out=BmR3[sl, :, 16 * m:16 * m + 16], in_=v3(BBR)[sl]),
           reads=[B_t], writes=[B_t])
        op("dve", lambda e, m=m, sl=sl: e.tensor_scalar(out=NBmI3[sl, :, 16 * m:16 * m + 16], in0=v3(BBI)[sl], scalar1=-1.0,
                                                      scalar2=None, op0=ALU.mult, op1=ALU.bypass),
           reads=[B_t], writes=[B_t])
    M32 = xin[0][:, 512:640]
    op("dve", lambda e: e.memset(M32, 0.0), reads=[B_t], writes=[B_t])
    for s in range(4):
        op("dve", lambda e, s=s: e.memset(M32[32 * s:32 * s + 32, 32 * s:32 * s + 32], 1.0), reads=[B_t], writes=[B_t])
    vp_mark = so["o"]
    VPr = salloc(512)
    VPi = salloc(512)
    VT1 = xin[0][:, 0:512]
    VT2 = xin[1][:, 0:512]
    KT = xin[0][:, 640:768]
    DG = xin[0][:, 768:896]
    g3 = lambda a: a.rearrange("p (q c) -> p q c", q=16)
    B_t2 = Buf("s5tmp2")
    fork = list(B_t.writers[-1:])
    WmR = scr[:, 5348:5860]
    WmI = xin[1][:, 512:1024]

    def tt2(out, a, b, o):
        op("pool", lambda e: e.tensor_tensor(out=out, in0=a, in1=b, op=o), reads=[B_t2], writes=[B_t2], deps=fork)

    op("pool", lambda e: e.memset(WmR, 0.0), reads=[B_t2], writes=[B_t2], deps=fork)
    op("pool", lambda e: e.memset(WmI, 0.0), reads=[B_t2], writes=[B_t2], deps=fork)

    def w_iter(j):
        n = 7 - j
        pr = pw(PWR, n).unsqueeze(2).to_broadcast([128, 16, 16])
        pi = pw(PWI, n).unsqueeze(2).to_broadcast([128, 16, 16])
        tt2(v3(U1), v3(BBR), pr, ALU.mult)
        tt2(v3(U2), v3(BBI), pi, ALU.mult)
        for m in range(2):
            sl = slice(64 * m, 64 * m + 64)
            tt2(g3(WmR)[sl, :, 16 * m:16 * m + 16], v3(U1)[sl], v3(U2)[sl], ALU.subtract)
        tt2(v3(U1), v3(BBR), pi, ALU.mult)
        tt2(v3(U2), v3(BBI), pr, ALU.mult)
        for m in range(2):
            sl = slice(64 * m, 64 * m + 64)
            tt2(g3(WmI)[sl, :, 16 * m:16 * m + 16], v3(U1)[sl], v3(U2)[sl], ALU.add)
        for ri, Wm in ((0, WmR), (1, WmI)):
            for T in range(4):
                bk = next_bank()
                op("pe", lambda e, Wm=Wm, T=T, bk=bk: e.transpose(bank(bk)[:, 0:128], Wm[:, 128 * T:128 * T + 128], idf),
                   reads=[B_t2, B_id], writes=[b_bank[bk]])
                op("act", lambda e, T=T, j=j, ri=ri, bk=bk: e.activation(out=WS[:, T, j, ri, :], in_=bank(bk)[:, 0:128], func=AF.Copy),
                   reads=[b_bank[bk]], writes=[B_const])

    for n in range(0, 9):
        if n < 8:
            w_iter(n)
        pr = pw(PWR, n).unsqueeze(2).to_broadcast([128, 16, 32])
        pi = pw(PWI, n).unsqueeze(2).to_broadcast([128, 16, 32])
        tt("dve", g3(VT1), CTr3, pr, ALU.mult)
        tt("dve", g3(VT2), CTi3, pi, ALU.mult)
        tt("dve", g3(VPr), g3(VT1), g3(VT2), ALU.subtract)
        tt("dve", g3(VT1), CTr3, pi, ALU.mult)
        tt("dve", g3(VT2), CTi3, pr, ALU.mult)
        tt("dve", g3(VPi), g3(VT1), g3(VT2), ALU.add)
        if n >= 1:
            j = n - 1
            op("dve", lambda e, j=j: e.tensor_copy(out=VV[:, :, j, 0, :], in_=g3(VPr)), reads=[B_t], writes=[B_const])
            op("dve", lambda e, j=j: e.tensor_scalar(out=VV[:, :, j, 1, :], in0=g3(VPi), scalar1=-1.0, scalar2=None,
                                                   op0=ALU.mult, op1=ALU.bypass), reads=[B_t], writes=[B_const])
        if n <= 7:
            tau = n
            for T in range(4):
                bk = next_bank()
                op("pe", lambda e, T=T, bk=bk: e.matmul(bank(bk)[:, 0:128], lhsT=BmR[:, 128 * T:128 * T + 128],
                                                       rhs=VPr[:, 128 * T:128 * T + 128], start=True, stop=False),
                   reads=[B_t], writes=[b_bank[bk]])
                op("pe", lambda e, T=T, bk=bk: e.matmul(bank(bk)[:, 0:128], lhsT=NBmI[:, 128 * T:128 * T + 128],
                                                       rhs=VPi[:, 128 * T:128 * T + 128], start=False, stop=True),
                   reads=[B_t], writes=[b_bank[bk]])
                if tau == 0:
                    op("dve", lambda e, bk=bk: e.tensor_tensor(out=KT, in0=bank(bk)[:, 0:128], in1=M32, op=ALU.mult),
                       reads=[b_bank[bk], B_t], writes=[B_t])
                    op("dve", lambda e, T=T: e.tensor_scalar(out=DG, in0=idf, scalar1=DS[:, T:T + 1], scalar2=None,
                                                           op0=ALU.mult, op1=ALU.bypass), reads=[B_t, B_s5, B_id], writes=[B_t])
                    op("dve", lambda e, T=T: e.tensor_tensor(out=Kc[:, T, 0, :], in0=KT, in1=DG, op=ALU.add),
                       reads=[B_t], writes=[B_const])
                else:
                    op("dve", lambda e, T=T, bk=bk, tau=tau: e.tensor_tensor(out=Kc[:, T, tau, :], in0=bank(bk)[:, 0:128],
                                                                          in1=M32, op=ALU.mult),
                       reads=[b_bank[bk], B_t], writes=[B_const])
    P.barrier()
    if mode == "setup":
        dbg = {"WsT": (WsT, [128, 1024]), "EC": (EC, [128, 1024]), "ES": (ES, [128, 1024]),
               "RHO": (RHO, [128, 16]), "WS": (WS, [128, 8192]), "VV": (VV, [128, 8192]), "Kc": (Kc, [128, 4096]),
               "lnb_bc": (lnb_bc, [128, 1024])}
        sts = []
        for name, (t, shp) in dbg.items():
            o = dt("dbg_" + name, shp, t.dtype, kind="ExternalOutput")
            nd = len(t.shape)
            src = t[:] if nd == 2 else t[:].rearrange({3: "p a b -> p (a b)", 4: "p a b c -> p (a b c)", 5: "p a b c d -> p (a b c d)"}[nd])
            sts.append(op("sp", lambda e, o=o, src=src: e.dma_start(out=o, in_=src), reads=[B_const], dma=True, sembuf=Buf()))
        op("sp", None, deps=sts)
        P.finalize()
        P.close()
        return nc

    ld = {"next": 0, "done": -1}

    def load_block(gp):
        blk = SEQ[gp % NBLK]
        slot = gp % NR
        w = 1024 if BLK_GLU <= blk < BLK_WO else 2048
        op("sp", lambda e: e.dma_start(out=ring[slot][:, 0:w], in_=wsc[blk, :, 0:w]), reads=[b_wblk[blk]], writes=[b_ring[slot]], dma=True)

    def advance(cur):
        while ld["next"] < nst * NBLK and ld["next"] - NR <= ld["done"] and ld["next"] <= cur + NR - 1:
            load_block(ld["next"])
            ld["next"] += 1

    def need(gp):
        advance(gp)
        assert ld["next"] > gp, (gp, ld)

    defer = {"on": False, "q": []}

    def done(gp):
        if defer["on"]:
            defer["q"].append(gp)
            return
        assert gp == ld["done"] + 1, (gp, ld)
        ld["done"] = gp
        advance(gp + 1)

    statn = {"i": 0}

    def stat_slot():
        i = statn["i"] % 32
        statn["i"] += 1
        return stat[:, 2 * i:2 * i + 1], stat[:, 2 * i + 1:2 * i + 2]

    def x_rows(ap, base):
        return ap[base:base + 128, :].rearrange("(c j) d -> j c d", j=8)

    def rmsnorm_rstd(src, junk, b_src, b_junk, tag):
        ss, rs = stat_slot()
        b = Buf(tag)
        op("act", lambda e: e.activation(out=junk, in_=src, func=AF.Square, accum_out=ss), reads=[b_src], writes=[b_junk, b])
        op("pool", lambda e: e.tensor_scalar(out=rs, in0=ss, scalar1=1.0 / D, scalar2=EPS, op0=ALU.mult, op1=ALU.add),
           reads=[b], writes=[b])
        op("pool", lambda e: e.tensor_tensor(out=rs, in0=rs, in1=mhalf[:], op=ALU.pow), reads=[b, B_const], writes=[b])
        return rs, b

    def proj_tile(blk_base, ct, nk, rhs_of, rhs_bufs, g0):
        gp = g0 + POS[blk_base + ct // 2]
        need(gp)
        slot = gp % NR
        bk = next_bank()
        for k in range(nk):
            op("pe", lambda e, k=k, slot=slot, bk=bk: e.matmul(bank(bk), lhsT=ring[slot][:, 256 * k + 128 * (ct % 2):256 * k + 128 * (ct % 2) + 128],
                                                            rhs=rhs_of(k), start=(k == 0), stop=(k == nk - 1)),
               reads=[b_ring[slot]] + rhs_bufs(k), writes=[b_bank[bk]])
        if ct % 2 == 1:
            done(gp)
        return bk

    def finish():
        op("sp", None, deps=P._reduce(P.out_stores) if P.out_stores else [])
        P.finalize()
        P.close()
        return nc

    def prepA_steps(st):
        tok0 = st * ST
        rsb = {}

        def load(C):
            s = C % 2
            base = tok0 + 128 * C
            op("act", lambda e, s=s, base=base: e.dma_start(out=xin[s][:], in_=x_rows(x_d, base)), writes=[b_xin[s]], dma=True)

        def square(C):
            s = C % 2
            rsb[C] = rmsnorm_rstd(xin[s][:], xs[C][:], b_xin[s], b_xs[C], "rs")

        def scale(C):
            s = C % 2
            rs, brs = rsb[C]
            op("act", lambda e, s=s, C=C, rs=rs: e.activation(out=xs[C][:], in_=xin[s][:], func=AF.Identity, scale=rs),
               reads=[b_xin[s], brs], writes=[b_xs[C]])

        return [lambda: (load(0), load(1)), lambda: (square(0), square(1)), lambda: (scale(0), load(2), scale(1), load(3)),
                lambda: (square(2), square(3)), lambda: (scale(2), scale(3))]

    def prepA(st):
        for f in prepA_steps(st):
            f()

    def prepB(st, only=None):
        xnT = xnT2[st % 2]
        b_xnT = b_xnT2[st % 2]
        for C in (range(4) if only is None else [only]):
            for k in range(8):
                op("pe", lambda e, C=C, k=k: e.transpose(psT[:, 128 * k:128 * k + 128], xs[C][:, 128 * k:128 * k + 128], idb[:]),
                   reads=[b_xs[C], B_const], writes=[b_psT])
            op("dve", lambda e, C=C, xnT=xnT: e.tensor_tensor(out=xnT[:, :, 128 * C:128 * C + 128],
                                                             in0=psT[:].rearrange("p (k t) -> p k t", k=8),
                                                             in1=gain_t[:].unsqueeze(2).to_broadcast([128, 8, 128]), op=ALU.mult),
               reads=[b_psT, B_const], writes=[b_xnT])

    prepA(0)
    prepB(0)
    for st in range(nst):
        g0 = st * NBLK
        tok0 = st * ST
        xnT = xnT2[st % 2]
        b_xnT = b_xnT2[st % 2]
        if upto == 1:
            return finish()
        pa_steps = prepA_steps(st + 1) if st + 1 < nst else None
        if pa_steps:
            pa_steps[0]()
        xr = lambda k, xnT_=xnT: xnT_[:, k, :]
        xb_ = lambda k, b_=b_xnT: [b_]
        gpv = g0 + POS[BLK_V]
        need(gpv + 3)
        def u_tile(ct):
            bk = proj_tile(BLK_U, ct, 8, xr, xb_, g0)
            op("act", lambda e, ct=ct, bk=bk: e.activation(out=gusz[:, ct, :], in_=bank(bk), func=AF.Gelu_apprx_tanh),
               reads=[b_bank[bk]], writes=[b_gusz[ct]])
            if pa_steps and ct % 2 == 0:
                pa_steps[1 + ct // 2]()

        defer["on"] = True
        for C in range(4):
            pb = next_pair()
            s = C % 2
            for vb in range(4):
                slot = (gpv + vb) % NR
                for k in range(8):
                    op("pe", lambda e, C=C, vb=vb, k=k, slot=slot, pb=pb, xnT_=xnT: e.matmul(
                        psA[:, 512 * pb + 256 * vb:512 * pb + 256 * vb + 256], lhsT=xnT_[:, k, 128 * C:128 * C + 128],
                        rhs=ring[slot][:, 256 * k:256 * k + 256], start=(k == 0), stop=(k == 7)),
                       reads=[b_xnT, b_ring[slot]], writes=[b_bank[pb], b_bank[pb + 1]])
            op("act", lambda e, s=s, pb=pb: e.activation(out=gv[s], in_=psA[:, 512 * pb:512 * pb + 1024], func=AF.Gelu_apprx_tanh),
               reads=[b_bank[pb], b_bank[pb + 1]], writes=[b_gv[s]])
            i6 = statn["i"] % 4
            statn["i"] += 1
            bnst = P_bn[i6]
            mv = P_mv[i6]
            bmv = Buf("mv")
            op("dve", lambda e, s=s, bnst=bnst: e.bn_stats(out=bnst[:, 0, :], in_=gv[s][:, 0:512]), reads=[b_gv[s]], writes=[bmv])
            op("dve", lambda e, s=s, bnst=bnst: e.bn_stats(out=bnst[:, 1, :], in_=gv[s][:, 512:1024]), reads=[b_gv[s]], writes=[bmv])
            op("dve", lambda e, bnst=bnst, mv=mv: e.bn_aggr(out=mv[:, 0:2], in_=bnst[:].rearrange("p a b -> p (a b)")),
               reads=[bmv], writes=[bmv])
            op("pool", lambda e, mv=mv: e.tensor_scalar(out=mv[:, 2:3], in0=mv[:, 1:2], scalar1=EPS, scalar2=None,
                                                      op0=ALU.add, op1=ALU.bypass), reads=[bmv], writes=[bmv])
            op("pool", lambda e, mv=mv: e.tensor_tensor(out=mv[:, 2:3], in0=mv[:, 2:3], in1=mhalf[:], op=ALU.pow),
               reads=[bmv, B_const], writes=[bmv])
            op("dve", lambda e, s=s, mv=mv: e.scalar_tensor_tensor(out=gv[s], in0=gv[s], scalar=mv[:, 0:1], in1=lng_bc[:],
                                                                   op0=ALU.subtract, op1=ALU.mult),
               reads=[bmv, b_gv[s], B_const], writes=[b_gv[s]])
            op("dve", lambda e, s=s, C=C, mv=mv: e.scalar_tensor_tensor(out=vn[:, C, :], in0=gv[s], scalar=mv[:, 2:3], in1=lnb_bc[:],
                                                                        op0=ALU.mult, op1=ALU.add),
               reads=[bmv, b_gv[s], B_const], writes=[b_vn[C]])
            if C in (1, 2):
                rr["avoid"] = {pb, pb + 1, prev_pb, prev_pb + 1}
                u_tile(2 * C - 2)
                u_tile(2 * C - 1)
                rr["avoid"] = None
            prev_pb = pb
        defer["on"] = False
        for vb in range(4):
            done(gpv + vb)
        for gp_q in defer["q"]:
            done(gp_q)
        defer["q"] = []

        xr = lambda k, xnT_=xnT: xnT_[:, k, :]
        xb_ = lambda k, b_=b_xnT: [b_]
        for ct in range(4, 8):
            u_tile(ct)
        if upto == 2:
            return finish()
        for ct in range(8):
            bk = proj_tile(BLK_ZA, ct, 8, xr, xb_, g0)
            s = ct % 2
            op("act", lambda e, s=s, bk=bk: e.activation(out=szt[s][:], in_=bank(bk), func=AF.Silu),
               reads=[b_bank[bk]], writes=[b_szt[s]])
            op("dve", lambda e, s=s, ct=ct: e.tensor_tensor(out=gusz[:, ct, :], in0=gusz[:, ct, :], in1=szt[s][:], op=ALU.mult),
               reads=[b_szt[s], b_gusz[ct]], writes=[b_gusz[ct]])
        if upto == 3:
            return finish()
        for ct in range(4):
            bk = proj_tile(BLK_XB, ct, 8, xr, xb_, g0)
            op("dve", lambda e, ct=ct, bk=bk: e.tensor_copy(out=ub[:, ct, :].rearrange("p (j c t) -> p c j t", j=8, c=4),
                                                          in_=bank(bk).rearrange("p (c j t) -> p c j t", c=4, j=8)),
               reads=[b_bank[bk]], writes=[b_ub[ct]])
        if upto == 4:
            return finish()
        for h in range(8):
            bk = next_bank()
            op("pe", lambda e, h=h, bk=bk: e.matmul(bank(bk), lhsT=ones1[0:1, :], rhs=bsrow[0:1, h, :], start=True, stop=False,
                                                   skip_group_check=True),
               reads=[B_const], writes=[b_bank[bk]])
            for C in range(4):
                op("pe", lambda e, h=h, C=C, bk=bk: e.matmul(bank(bk)[:, 128 * C:128 * C + 128], lhsT=vn[:, C, 128 * h:128 * h + 128],
                                                            rhs=WsT[:, h, :], start=False, stop=(C == 3), skip_group_check=True),
                   reads=[b_vn[C], B_const], writes=[b_bank[bk]])
            op("dve", lambda e, h=h, bk=bk: e.tensor_tensor(out=gusz[:, h, :], in0=bank(bk), in1=gusz[:, h, :], op=ALU.mult),
               reads=[b_bank[bk], b_gusz[h]], writes=[b_gusz[h]])
        if upto == 5:
            return finish()
        if st % SEQ_ST == 0:
            for G in range(4):
                op("pool", lambda e, G=G: e.memset(carry[:, 4 * G:4 * G + 4, :], 0.0), writes=[b_carry[G]])
        sb_bufs = [b_bank[i] for i in range(4)]
        for G in range(4):
            for ri in range(2):
                for j in range(8):
                    for s in range(4):
                        c0 = 128 * G + 64 * ri
                        op("pe", lambda e, G=G, s=s, ri=ri, j=j, c0=c0: e.matmul(
                            bank(s)[:, c0:c0 + 64],
                            lhsT=WS[32 * s:32 * s + 32, G, j, ri, :],
                            rhs=ub[32 * s:32 * s + 32, G, 64 * j:64 * j + 64],
                            start=(j == 0), stop=(j == 7), tile_position=(32 * s, 0), skip_group_check=True),
                           reads=[b_ub[G], B_const], writes=[b_bank[s]])
        rr["b"] = 4
        rr["hi"] = 12
        for G in range(4):
            S4 = psA[:, 0:2048].rearrange("p (s g r c) -> p s g r c", s=4, g=4, r=2)[:, :, G, :, :]
            ecb = EC[:, 4 * G:4 * G + 4, :].unsqueeze(2).to_broadcast([128, 4, 2, 64])
            esb = ES[:, 4 * G:4 * G + 4, :].unsqueeze(2).to_broadcast([128, 4, 2, 64])
            v4 = lambda a: a.rearrange("p (s r c) -> p s r c", s=4, r=2)
            Ta, Tb, Gi, Hf = s5t
            bTa, bTb, bGi, bHf = b_s5t
            Gt = gt2[G % 2][:]
            bGt = b_gt2[G % 2]
            Ua, Ub = [t[:] for t in s5u]
            bUa, bUb = b_s5u
            op("dve", lambda e, S4=S4, ecb=ecb: e.tensor_tensor(out=v4(Ta), in0=S4, in1=ecb, op=ALU.mult),
               reads=sb_bufs + [B_const], writes=[bTa])
            op("dve", lambda e, S4=S4, esb=esb: e.tensor_tensor(out=v4(Tb), in0=S4, in1=esb, op=ALU.mult),
               reads=sb_bufs + [B_const], writes=[bTb])
            op("dve", lambda e: e.tensor_tensor(out=v4(Gi)[:, :, 0, :], in0=v4(Ta)[:, :, 0, :], in1=v4(Tb)[:, :, 1, :], op=ALU.add),
               reads=[bTa, bTb], writes=[bGi])
            op("dve", lambda e: e.tensor_tensor(out=v4(Gi)[:, :, 1, :], in0=v4(Ta)[:, :, 1, :], in1=v4(Tb)[:, :, 0, :], op=ALU.subtract),
               reads=[bTa, bTb], writes=[bGi])
            for s in range(4):
                q = 4 * G + s
                for ri in range(2):
                    op("dve", lambda e, s=s, ri=ri, q=q, Gt=Gt: e.tensor_tensor_scan(
                        out=v4(Gt)[:, s, ri, :], data0=RHO[:, q:q + 1].to_broadcast([128, 64]), data1=v4(Gi)[:, s, ri, :],
                        initial=carry[:, q, ri:ri + 1], op0=ALU.mult, op1=ALU.add),
                       reads=[bGi, B_const, b_carry[G]], writes=[bGt])
            op("pool", lambda e, ecb=ecb, Gt=Gt: e.tensor_tensor(out=v4(Ua), in0=v4(Gt), in1=ecb, op=ALU.mult),
               reads=[bGt, B_const], writes=[bUa])
            op("pool", lambda e, esb=esb, Gt=Gt: e.tensor_tensor(out=v4(Ub), in0=v4(Gt), in1=esb, op=ALU.mult),
               reads=[bGt, B_const], writes=[bUb])
            op("pool", lambda e: e.tensor_tensor(out=v4(Hf)[:, :, 0, :], in0=v4(Ua)[:, :, 0, :], in1=v4(Ub)[:, :, 1, :], op=ALU.subtract),
               reads=[bUa, bUb], writes=[bHf])
            op("pool", lambda e: e.tensor_tensor(out=v4(Hf)[:, :, 1, :], in0=v4(Ua)[:, :, 1, :], in1=v4(Ub)[:, :, 0, :], op=ALU.add),
               reads=[bUa, bUb], writes=[bHf])
            op("pool", lambda e, G=G: e.tensor_copy(out=hsh[:, 4 * G:4 * G + 4, :, 0:1], in_=carry[:, 4 * G:4 * G + 4, :].unsqueeze(3)),
               reads=[b_carry[G]], writes=[b_hsh[G]])
            op("pool", lambda e, G=G: e.tensor_copy(out=hsh[:, 4 * G:4 * G + 4, :, 1:64], in_=v4(Hf)[:, :, :, 0:63]),
               reads=[bHf], writes=[b_hsh[G]])
            op("pool", lambda e, G=G: e.tensor_copy(out=carry[:, 4 * G:4 * G + 4, :].unsqueeze(3), in_=v4(Hf)[:, :, :, 63:64]),
               reads=[bHf], writes=[b_carry[G]])
        if upto == 6:
            return finish()
        for ct in range(4):
            bk = proj_tile(BLK_ZB, ct, 8, xr, xb_, g0)
            op("act", lambda e, ct=ct, bk=bk: e.activation(out=szb[:, ct, :], in_=bank(bk), func=AF.Silu),
               reads=[b_bank[bk]], writes=[b_szb[ct]])
        for ct in range(8):
            bk = proj_tile(BLK_GA, ct, 8, xr, xb_, g0)
            op("act", lambda e, ct=ct, bk=bk: e.activation(out=tga[:, ct, :], in_=bank(bk), func=AF.Tanh, scale=0.5),
               reads=[b_bank[bk]], writes=[b_tga[ct]])
        for ct in range(8):
            bk = proj_tile(BLK_AD, ct, 8, lambda k: gusz[:, k, :], lambda k: [b_gusz[k]], g0)
            op("dve", lambda e, ct=ct, bk=bk: e.scalar_tensor_tensor(out=yag[:, ct, :], in0=tga[:, ct, :], scalar=1.0,
                                                                    in1=bank(bk), op0=ALU.add, op1=ALU.mult),
               reads=[b_tga[ct], b_bank[bk]], writes=[b_yag[ct]])
        for T in range(4):
            bk = next_bank()
            for tau in range(8):
                w = 16 * (8 - tau)
                op("pe", lambda e, T=T, tau=tau, w=w, bk=bk: e.matmul(
                    bank(bk)[:, 64 * tau:512],
                    lhsT=Kc[:, T, tau, :], rhs=ub[:, T, 0:4 * w],
                    start=(tau == 0), stop=False, skip_group_check=True),
                   reads=[b_ub[T], B_const], writes=[b_bank[bk]])
            for j in range(8):
                for ri in range(2):
                    for s in range(4):
                        q = 4 * T + s
                        last = (ri == 1)
                        op("pe", lambda e, T=T, s=s, q=q, j=j, ri=ri, bk=bk, last=last: e.matmul(
                            bank(bk)[32 * s:32 * s + 32, 64 * j:64 * j + 64],
                            lhsT=VV[:, q, j, ri, :], rhs=hsh[:, q, ri, :],
                            start=False, stop=last, tile_position=(0, 32 * s), skip_group_check=True),
                           reads=[b_hsh[T], B_const], writes=[b_bank[bk]])
            op("act", lambda e, T=T, bk=bk: e.activation(out=gy[:, T, :].rearrange("p (c j t) -> p c j t", c=4, j=8),
                                                         in_=bank(bk).rearrange("p (j c t) -> p c j t", j=8, c=4),
                                                         func=AF.Gelu_apprx_tanh),
               reads=[b_bank[bk]], writes=[b_gy[T]])
        if upto == 7:
            return finish()
        gpg = g0 + POS[BLK_GLU + 2]
        gyr = lambda k: gy[:, k, :]
        gyb = lambda k: [b_gy[k]]
        for ct in range(4):
            s = ct % 2
            gq = g0 + POS[BLK_GLU + 2 + ct // 2]
            gp_ = g0 + POS[BLK_GLU + ct // 2]
            need(gq)
            need(gp_)
            bk = next_bank()
            for k in range(4):
                op("pe", lambda e, k=k, ct=ct, gq=gq, bk=bk: e.matmul(
                    bank(bk), lhsT=ring[gq % NR][:, 256 * k + 128 * (ct % 2):256 * k + 128 * (ct % 2) + 128], rhs=gy[:, k, :],
                    start=(k == 0), stop=(k == 3)), reads=[b_ring[gq % NR], b_gy[k]], writes=[b_bank[bk]])
            op("act", lambda e, s=s, bk=bk: e.activation(out=th[s][:], in_=bank(bk), func=AF.Tanh, scale=0.5),
               reads=[b_bank[bk]], writes=[b_th[s]])
            bk2 = next_bank()
            for k in range(4):
                op("pe", lambda e, k=k, ct=ct, gp_=gp_, bk2=bk2: e.matmul(
                    bank(bk2), lhsT=ring[gp_ % NR][:, 256 * k + 128 * (ct % 2):256 * k + 128 * (ct % 2) + 128], rhs=gy[:, k, :],
                    start=(k == 0), stop=(k == 3)), reads=[b_ring[gp_ % NR], b_gy[k]], writes=[b_bank[bk2]])
            op("dve", lambda e, s=s, bk2=bk2: e.scalar_tensor_tensor(out=tq[s][:], in0=th[s][:], scalar=1.0, in1=bank(bk2),
                                                                   op0=ALU.add, op1=ALU.mult),
               reads=[b_th[s], b_bank[bk2]], writes=[b_tq[s]])
            op("pool", lambda e, s=s, ct=ct: e.tensor_tensor(out=ybp[:, ct, :], in0=tq[s][:], in1=szb[:, ct, :], op=ALU.mult),
               reads=[b_tq[s], b_szb[ct]], writes=[b_ybp[ct]])
            if ct % 2 == 1:
                done(gq)
                done(gp_)
        if upto == 8:
            return finish()
        def o_load(C):
            s = C % 2
            base = tok0 + 128 * C
            op("act", lambda e, s=s, base=base: e.dma_start(out=xres[s][:], in_=x_rows(x_d, base)), writes=[b_xres[s]], dma=True)

        o_load(0)
        o_load(1)
        gpb = g0 + POS[BLK_GB]
        gpbd = g0 + POS[BLK_BD]
        for half in range(2):
            for c4 in range(4):
                ct = 4 * half + c4
                bk = proj_tile(BLK_GB, ct, 8, xr, xb_, g0)
                op("act", lambda e, c4=c4, bk=bk: e.activation(out=tga[:, c4, :], in_=bank(bk), func=AF.Tanh, scale=0.5),
                   reads=[b_bank[bk]], writes=[b_tga[c4]])
            for c4 in range(4):
                ct = 4 * half + c4
                bk = proj_tile(BLK_BD, ct, 4, lambda k: ybp[:, k, :], lambda k: [b_ybp[k]], g0)
                s = ct % 2
                op("dve", lambda e, s=s, c4=c4, bk=bk: e.scalar_tensor_tensor(out=tyb[s][:], in0=tga[:, c4, :], scalar=1.0,
                                                                            in1=bank(bk), op0=ALU.add, op1=ALU.mult),
                   reads=[b_tga[c4], b_bank[bk]], writes=[b_tyb[s]])
                op("dve", lambda e, s=s, ct=ct: e.scalar_tensor_tensor(out=yag[:, ct, :], in0=tyb[s][:], scalar=0.5, in1=yag[:, ct, :],
                                                                        op0=ALU.mult, op1=ALU.add),
                   reads=[b_tyb[s], b_yag[ct]], writes=[b_yag[ct]])
        if upto == 9:
            return finish()
        gpo = g0 + POS[BLK_WO]
        need(gpo + 3)
        ors = {}

        def o_stage1(C):
            s = C % 2
            if st + 1 < nst:
                prepB(st + 1, only=C)
            pb = next_pair()
            for ob in range(4):
                slot = (gpo + ob) % NR
                for k in range(8):
                    op("pe", lambda e, C=C, ob=ob, k=k, slot=slot, pb=pb: e.matmul(
                        psA[:, 512 * pb + 256 * ob:512 * pb + 256 * ob + 256], lhsT=yag[:, k, 128 * C:128 * C + 128],
                        rhs=ring[slot][:, 256 * k:256 * k + 256], start=(k == 0), stop=(k == 7)),
                       reads=[b_yag[k], b_ring[slot]], writes=[b_bank[pb], b_bank[pb + 1]])
            op("dve", lambda e, s=s, pb=pb: e.scalar_tensor_tensor(out=xres[s][:], in0=psA[:, 512 * pb:512 * pb + 1024], scalar=0.5,
                                                                   in1=xres[s][:], op0=ALU.mult, op1=ALU.add),
               reads=[b_bank[pb], b_bank[pb + 1], b_xres[s]], writes=[b_xres[s]])
            ors[C] = rmsnorm_rstd(xres[s][:], xs[C][:], b_xres[s], b_xs[C], "rs2")

        def o_stage2(C):
            s = C % 2
            base = tok0 + 128 * C
            rs, brs = ors[C]
            op("dve", lambda e, s=s, rs=rs: e.scalar_tensor_tensor(out=ost[s], in0=xres[s][:], scalar=rs, in1=fg_bc[:],
                                                                   op0=ALU.mult, op1=ALU.mult),
               reads=[b_xres[s], brs, B_const], writes=[b_ost[s]])
            P.out_stores.append(op("pool", lambda e, s=s, base=base: e.dma_start(out=x_rows(y_d, base), in_=ost[s]),
                                   reads=[b_ost[s]], dma=True))

        o_stage1(0)
        o_stage2(0)
        o_load(2)
        o_stage1(1)
        o_stage2(1)
        o_load(3)
        o_stage1(2)
        o_stage2(2)
        o_stage1(3)
        o_stage2(3)
        for ob in range(4):
            done(gpo + ob)

    op("sp", None, deps=P._reduce(P.out_stores))
    P.finalize()
    P.close()
    return nc


_CACHE = {}


def kernel(**inputs):
    x = np.ascontiguousarray(np.asarray(inputs["x"], dtype=np.float32))
    B, S, Dm = x.shape
    xs = x.reshape(NCORES, TOK, Dm)
    sq = lambda k: np.ascontiguousarray(np.asarray(inputs[k], dtype=np.float32)[0])
    shared = {k: sq(k) for k in ("norm_gain", "w_in", "a_ln_gain", "a_ln_bias", "a_spatial", "a_spatial_bias", "w_a_down",
                                 "lambda_re", "lambda_im", "log_dt", "b_re", "b_im", "c_re", "c_im", "d_skip", "w_glu",
                                 "w_b_down", "w_out")}
    shared["final_gain"] = np.ascontiguousarray(np.asarray(inputs["final_gain"], dtype=np.float32))
    if "nc" not in _CACHE:
        _CACHE["nc"] = build_program()
    nc = _CACHE["nc"]
    in_maps = [dict(shared, x=np.ascontiguousarray(xs[i])) for i in range(NCORES)]
    res = run_bass_kernel_spmd(nc, in_maps, core_ids=list(range(NCORES)))
    out = np.stack([np.asarray(r["y"]) for r in res.results], axis=0)
    return out.reshape(B, S, Dm).astype(np.float32)
```
